# Optimizing a Trainium2 kernel written in Bass

```python
import math
import jax, jax.numpy as jnp
from jax import lax
import numpy as np

D_MODEL = 1024
BATCH = 4
SEQ = 8192
DEPTH = 4

N_MIXERS = 2
CHUNK = 128
SG_WIDTH = 2 * D_MODEL
SG_GROUPS = 8
SG_GROUP_DIM = SG_WIDTH // SG_GROUPS
RET_HEADS = 4
RET_DK = 256
RET_DV = 512
RET_QK_WIDTH = RET_HEADS * RET_DK
RET_V_WIDTH = RET_HEADS * RET_DV
RET_IN_WIDTH = 2 * RET_QK_WIDTH + 2 * RET_V_WIDTH
ROPE_BASE = 10000.0
FFN_WIDTH = 4 * D_MODEL
N_MOD = 6
N_A_LAYERS = (DEPTH + 1) // 2
N_B_LAYERS = DEPTH // 2

kernel_name = "hybrid_gmlp_retention_adaln_trunk"


def rms_norm(x, g, eps=1e-6):
    xf = x.astype(jnp.float32)
    y = xf * lax.rsqrt(jnp.mean(xf * xf, axis=-1, keepdims=True) + eps)
    return (y * g.astype(jnp.float32)).astype(x.dtype)


def layer_norm(x, g, b, eps=1e-5):
    xf = x.astype(jnp.float32)
    mu = jnp.mean(xf, axis=-1, keepdims=True)
    var = jnp.mean(jnp.square(xf - mu), axis=-1, keepdims=True)
    y = (xf - mu) * lax.rsqrt(var + eps)
    return (y * g.astype(jnp.float32) + b.astype(jnp.float32)).astype(x.dtype)


def spatial_gating_mixer(h, w_in, ln_g, ln_b, w_s, b_s, w_out):
    B, S, _ = h.shape
    nc = S // CHUNK
    z = jax.nn.gelu(h @ w_in)
    u, v = jnp.split(z, 2, axis=-1)
    v = layer_norm(v, ln_g, ln_b)
    v = v.reshape(B, nc, CHUNK, SG_GROUPS, SG_GROUP_DIM)
    w_causal = jnp.tril(w_s).astype(v.dtype)
    s = jnp.einsum('gtp,bnpgc->bntgc', w_causal, v) + b_s.T[:, :, None].astype(v.dtype)
    y = u * s.reshape(B, S, SG_WIDTH)
    return y @ w_out


def rotary(t, pos):
    half = t.shape[-1] // 2
    inv_freq = ROPE_BASE ** (-jnp.arange(half, dtype=jnp.float32) / half)
    ang = pos[:, None] * inv_freq[None, :]
    cos = jnp.cos(ang)[None, :, None, :]
    sin = jnp.sin(ang)[None, :, None, :]
    tf = t.astype(jnp.float32)
    t1, t2 = tf[..., :half], tf[..., half:]
    return jnp.concatenate([t1 * cos - t2 * sin, t1 * sin + t2 * cos], axis=-1)


def chunkwise_retention(q, k, v):
    B, S, H, DK = q.shape
    DV = v.shape[-1]
    nc = S // CHUNK

    def to_chunks(t):
        return t.reshape(B, nc, CHUNK, H, t.shape[-1]).transpose(1, 0, 3, 2, 4)

    log_gamma = jnp.log(1.0 - jnp.exp2(-5.0 - jnp.arange(H, dtype=jnp.float32)))
    idx = jnp.arange(CHUNK, dtype=jnp.float32)
    diff = idx[:, None] - idx[None, :]
    decay_intra = jnp.where(diff >= 0, jnp.exp(log_gamma[:, None, None] * jnp.maximum(diff, 0.0)), 0.0)
    decay_q = jnp.exp(log_gamma[:, None] * (idx + 1.0))
    decay_k = jnp.exp(log_gamma[:, None] * (CHUNK - 1.0 - idx))
    decay_chunk = jnp.exp(log_gamma * CHUNK)

    def step(state, qkv):
        qc, kc, vc = qkv
        scores = jnp.einsum('bhnd,bhmd->bhnm', qc, kc) * decay_intra
        out = (jnp.einsum('bhnm,bhme->bhne', scores, vc)
               + jnp.einsum('bhnd,bhde->bhne', qc, state) * decay_q[:, :, None])
        state = (state * decay_chunk[:, None, None]
                 + jnp.einsum('bhmd,bhme->bhde', kc * decay_k[:, :, None], vc))
        return state, out

    init = jnp.zeros((B, H, DK, DV), jnp.float32)
    _, out = lax.scan(step, init, (to_chunks(q), to_chunks(k), to_chunks(v)))
    return out.transpose(1, 0, 3, 2, 4).reshape(B, S, H, DV)


def retention_mixer(h, w_in, gn_g, gn_b, w_out):
    B, S, _ = h.shape
    proj = h @ w_in
    q, k, v, g = jnp.split(proj, [RET_QK_WIDTH, 2 * RET_QK_WIDTH, 2 * RET_QK_WIDTH + RET_V_WIDTH], axis=-1)
    pos = jnp.arange(S, dtype=jnp.float32)
    q = rotary(q.reshape(B, S, RET_HEADS, RET_DK), pos)
    k = rotary(k.reshape(B, S, RET_HEADS, RET_DK), pos) * (RET_DK ** -0.5)
    v = v.reshape(B, S, RET_HEADS, RET_DV).astype(jnp.float32)
    o = chunkwise_retention(q, k, v)
    mu = jnp.mean(o, axis=-1, keepdims=True)
    var = jnp.mean(jnp.square(o - mu), axis=-1, keepdims=True)
    o = ((o - mu) * lax.rsqrt(var + 1e-5)).reshape(B, S, RET_V_WIDTH)
    o = o * gn_g.astype(jnp.float32) + gn_b.astype(jnp.float32)
    o = o.astype(h.dtype) * jax.nn.silu(g)
    return o @ w_out


def squared_relu_mlp(h, w_in, w_out):
    return jnp.square(jax.nn.relu(h @ w_in)) @ w_out


def setup_inputs(seed: int = 0) -> dict:
    key = jax.random.key(seed)
    ks = jax.random.split(key, 20)
    f32 = jnp.float32
    D = D_MODEL
    nrm = lambda k, shape, s: jax.random.normal(k, shape, f32) * s
    gate_offset = jnp.concatenate([jnp.zeros((2 * D,), f32), jnp.ones((D,), f32),
                                   jnp.zeros((2 * D,), f32), jnp.ones((D,), f32)])
    return {
        "x": nrm(ks[0], (BATCH, SEQ, D), 1.0),
        "c": nrm(ks[1], (BATCH, D), 1.0),
        "ada_w": nrm(ks[2], (DEPTH, D, N_MOD * D), 0.1 * D ** -0.5),
        "ada_b": nrm(ks[3], (DEPTH, N_MOD * D), 0.02) + gate_offset,
        "pre_mix_g": 1.0 + nrm(ks[4], (DEPTH, D), 0.02),
        "post_mix_g": 1.0 + nrm(ks[5], (DEPTH, D), 0.02),
        "pre_ffn_g": 1.0 + nrm(ks[6], (DEPTH, D), 0.02),
        "post_ffn_g": 1.0 + nrm(ks[7], (DEPTH, D), 0.02),
        "ffn_w_in": nrm(ks[8], (DEPTH, D, FFN_WIDTH), D ** -0.5),
        "ffn_w_out": nrm(ks[9], (DEPTH, FFN_WIDTH, D), FFN_WIDTH ** -0.5),
        "sg_w_in": nrm(ks[10], (N_A_LAYERS, D, 2 * SG_WIDTH), D ** -0.5),
        "sg_ln_g": 1.0 + nrm(ks[11], (N_A_LAYERS, SG_WIDTH), 0.02),
        "sg_ln_b": nrm(ks[12], (N_A_LAYERS, SG_WIDTH), 0.02),
        "sg_w_s": nrm(ks[13], (N_A_LAYERS, SG_GROUPS, CHUNK, CHUNK), 0.5 * CHUNK ** -0.5),
        "sg_b_s": 1.0 + nrm(ks[14], (N_A_LAYERS, SG_GROUPS, CHUNK), 0.1),
        "sg_w_out": nrm(ks[15], (N_A_LAYERS, SG_WIDTH, D), SG_WIDTH ** -0.5),
        "ret_w_in": nrm(ks[16], (N_B_LAYERS, D, RET_IN_WIDTH), D ** -0.5),
        "ret_gn_g": 1.0 + nrm(ks[17], (N_B_LAYERS, RET_V_WIDTH), 0.02),
        "ret_gn_b": nrm(ks[18], (N_B_LAYERS, RET_V_WIDTH), 0.02),
        "ret_w_out": nrm(ks[19], (N_B_LAYERS, RET_V_WIDTH, D), RET_V_WIDTH ** -0.5),
    }


def reference(x, c, ada_w, ada_b, pre_mix_g, post_mix_g, pre_ffn_g, post_ffn_g, ffn_w_in, ffn_w_out,
              sg_w_in, sg_ln_g, sg_ln_b, sg_w_s, sg_b_s, sg_w_out,
              ret_w_in, ret_gn_g, ret_gn_b, ret_w_out):
    c_act = jax.nn.silu(c)
    for i in range(DEPTH):
        mod = c_act @ ada_w[i] + ada_b[i]
        shift_m, scale_m, gate_m, shift_f, scale_f, gate_f = [m[:, None, :] for m in jnp.split(mod, N_MOD, axis=-1)]
        h = rms_norm(x, pre_mix_g[i]) * (1.0 + scale_m) + shift_m
        j = i // N_MIXERS
        if i % N_MIXERS == 0:
            y = spatial_gating_mixer(h, sg_w_in[j], sg_ln_g[j], sg_ln_b[j], sg_w_s[j], sg_b_s[j], sg_w_out[j])
        else:
            y = retention_mixer(h, ret_w_in[j], ret_gn_g[j], ret_gn_b[j], ret_w_out[j])
        x = x + gate_m * rms_norm(y, post_mix_g[i])
        h = rms_norm(x, pre_ffn_g[i]) * (1.0 + scale_f) + shift_f
        y = squared_relu_mlp(h, ffn_w_in[i], ffn_w_out[i])
        x = x + gate_f * rms_norm(y, post_ffn_g[i])
    return x
```

```python
import numpy as np
import concourse.bass as bass
import concourse.mybir as mybir
from concourse.bass_utils import run_bass_kernel_spmd

F32 = mybir.dt.float32
BF16 = mybir.dt.bfloat16
I32 = mybir.dt.int32
AF = mybir.ActivationFunctionType
ALU = mybir.AluOpType

D = 1024
KT = 8
SEQ = 8192
NB = 4
DEPTH = 4
TOK = 4096
CH = 128
NCORES = 8
FFN = 4096
H = 4
DK = 256
DV = 512
GAMMA = [1.0 - 2.0 ** (-5.0 - h) for h in range(H)]


class Buf:
    __slots__ = ("name", "w", "r")

    def __init__(self, name=""):
        self.name = name
        self.w = None
        self.r = []


class Op:
    __slots__ = ("eng", "fn", "deps", "signal", "sval", "dma", "dsem", "dval", "dprev", "cc")

    def __init__(self, eng, fn, dma):
        self.eng = eng
        self.fn = fn
        self.deps = []
        self.signal = False
        self.sval = 0
        self.dma = dma
        self.dsem = None
        self.dval = 0
        self.dprev = 0
        self.cc = False


ENGS = ["pe", "act", "dve", "pool", "sp"]


class Sched:
    def __init__(self, nc):
        self.nc = nc
        self.ops = {e: [] for e in ENGS}
        self.last = {e: None for e in ENGS}
        self.all_dmas = []

    dry = False
    _dummy = None

    def add(self, eng, fn, r=(), w=(), dma=False, extra=()):
        if self.dry:
            if Sched._dummy is None:
                Sched._dummy = Op("pe", None, False)
            return Sched._dummy
        op = Op(eng, fn, dma)
        deps = []
        for b in r:
            if b.w is not None:
                deps.append(b.w)
        for b in w:
            if b.w is not None:
                deps.append(b.w)
            lastr = {}
            for o in b.r:
                if o.dma:
                    deps.append(o)
                else:
                    lastr[o.eng] = o
            deps.extend(lastr.values())
        deps.extend(extra)
        seen = set()
        for d in deps:
            if d is op or id(d) in seen:
                continue
            seen.add(id(d))
            if (not d.dma) and d.eng == eng:
                if eng == "pe":
                    continue
            op.deps.append(d)
            d.signal = True
        for b in r:
            b.r.append(op)
        for b in w:
            b.w = op
            b.r = []
        self.ops[eng].append(op)
        if fn is not None:
            self.last[eng] = op
        if dma:
            self.all_dmas.append(op)
        return op

    def barrier(self):
        if self.dry:
            return
        lasts = [o for o in self.last.values() if o is not None] + list(self.all_dmas)
        self.all_dmas = []
        for e in ENGS:
            self.add(e, None, extra=lasts)

    def emit(self, stack, final_waits):
        nc = self.nc
        esem = {e: stack.enter_context(nc.semaphore("sem_" + e)) for e in ENGS}
        NDS = 24
        dsems = {e: [stack.enter_context(nc.semaphore("dsem_%s_%d" % (e, i))) for i in range(NDS)]
                 for e in ("sp", "pool", "act")}
        for e in ENGS:
            cnt = 0
            for op in self.ops[e]:
                if not op.dma and op.signal:
                    cnt += 1
                    op.sval = cnt
        ccsem = stack.enter_context(nc.semaphore("sem_cc"))
        ccval = 0
        dcount = {e: 0 for e in dsems}
        dval = {e: [0] * NDS for e in dsems}
        for e in dsems:
            for op in self.ops[e]:
                if op.cc:
                    op.dsem = ccsem
                    op.dprev = ccval
                    ccval += 1
                    op.dval = ccval
                elif op.dma:
                    i = dcount[e] % NDS
                    dcount[e] += 1
                    op.dsem = dsems[e][i]
                    op.dprev = dval[e][i]
                    dval[e][i] += 16
                    op.dval = dval[e][i]
        block = stack.enter_context(nc.Block())
        engobj = {"pe": block.tensor, "act": block.scalar, "dve": block.vector, "pool": block.gpsimd,
                  "sp": block.sync}

        def run(e):
            def body(eng):
                waited = {}
                semobj = {}
                for op in self.ops[e]:
                    needs = {}
                    for d in op.deps:
                        if d.dma:
                            s, v = d.dsem, d.dval
                        else:
                            s, v = esem[d.eng], d.sval
                        k = id(s)
                        semobj[k] = s
                        if needs.get(k, 0) < v:
                            needs[k] = v
                    if op.dma and op.dprev > 0:
                        k = id(op.dsem)
                        semobj[k] = op.dsem
                        if needs.get(k, 0) < op.dprev:
                            needs[k] = op.dprev
                    for k, v in needs.items():
                        if waited.get(k, 0) < v:
                            eng.wait_ge(semobj[k], v)
                            waited[k] = v
                    if op.fn is None:
                        continue
                    ins = op.fn(eng)
                    if op.cc:
                        ins.then_inc(op.dsem, 1)
                    elif op.dma:
                        ins.then_inc(op.dsem, 16)
                    elif op.signal:
                        ins.then_inc(esem[e], 1)
                if e == "sp":
                    for op in final_waits:
                        eng.wait_ge(op.dsem, op.dval)
            return body

        for e in ENGS:
            if self.ops[e] or e == "sp":
                engobj[e](run(e))


from contextlib import ExitStack

TT_OF = {"F": 1024, "A": 512, "B": 512, "S": 512}
RING_SLOTS = 4
UNIT = 4096


def units_of(kind):
    return {"F": 16, "A": 12, "B": 16, "S": 6}[kind]


UNIT_ORDER = {"F": list(range(16)), "A": [4, 5, 6, 7, 0, 1, 2, 3, 8, 9, 10, 11], "B": list(range(16)),
              "S": list(range(6))}


class Arena:
    def __init__(self, nc, nbytes):
        self.t = nc.alloc_sbuf_tensor("arena", [128, nbytes // 4], F32)
        self.off = 0
        self.cap = nbytes

    def mark(self):
        return self.off

    def reset(self, m):
        self.off = m

    def view(self, shape, dt):
        esz = 2 if dt == BF16 else 4
        n = 1
        for s_ in shape[1:]:
            n *= s_
        nbytes = (n * esz + 31) // 32 * 32
        assert self.off + nbytes <= self.cap, ("arena overflow", self.off, nbytes, self.cap)
        ap = self.t[0:shape[0], self.off // 4: self.off // 4 + nbytes // 4]
        self.off += nbytes
        if dt != F32:
            ap = ap.bitcast(dt)
        ap = ap[:, 0:n]
        if len(shape) == 3:
            ap = ap.rearrange("p (a b) -> p a b", a=shape[1])
        elif len(shape) == 4:
            ap = ap.rearrange("p (a b c) -> p a b c", a=shape[1], b=shape[2])
        return ap


def build_program(passes, ntok=TOK, first_from_input=True, dbg=None, fused=False):
    if fused:
        nc = bass.Bass("TRN2", target_bir_lowering=False, num_devices=NCORES)
    else:
        nc = bass.Bass("TRN2", target_bir_lowering=False)
    S = Sched(nc)
    stack = ExitStack()
    kinds = [k for k, _ in passes]

    def din(name, shape, dt=F32):
        return nc.dram_tensor(name, list(shape), dt, kind="ExternalInput").ap()

    def dout(name, shape, dt=F32):
        return nc.dram_tensor(name, list(shape), dt, kind="ExternalOutput").ap()

    xT_d = din("xT", [D, ntok])
    outT_d = dout("outT", [D, ntok])
    cT_d = din("cT", [128, KT])
    adaw_d = din("adaw", [DEPTH, 8, 128, KT * 768])
    adab_d = din("adab", [DEPTH, 128, 48])
    gains_d = din("gains", [DEPTH, 128, 32])
    ident_d = din("ident", [128, 128])
    wd = []
    wbd = []
    for pi, (k, l) in enumerate(passes):
        wd.append(din("w%d" % pi, [units_of(k), 128, UNIT]))
        wbd.append(nc.dram_tensor("wb%d" % pi, [units_of(k), 128, UNIT], BF16, kind="Internal").ap())
    B_wb = [[Buf("wb%d_%d" % (pi, u)) for u in range(units_of(k))] for pi, (k, l) in enumerate(passes)]
    pend_pc = []
    hasA = "A" in kinds
    hasB = ("B" in kinds) or ("S" in kinds)
    if hasA:
        wst_d = din("wst", [2, 128, 8 * 128])
        maskT_d = din("maskT", [128, 128])
        lngT_d = din("lngT", [2, 128, 16])
        lnbT_d = din("lnbT", [2, 128, 16])
        bsj_d = din("bsj", [2, 128, 16 * 128])
    if hasB:
        cos_d = din("cosT", [128, ntok])
        sin_d = din("sinT", [128, ntok])
        dmask_d = din("dmask", [128, H * 128])
        epsp_d = din("epsp", [128, H])
        dk_d = din("dk", [128, H])
        gng_d = din("gng", [2, 128, 16])
        gnb_d = din("gnb", [2, 128, 16])
        flag_d = din("flag", [128, 1])
        if not fused:
            st_in_d = din("st_in", [128, 8 * 512])
            st_out_d = dout("st_out", [128, 8 * 512])
        xch = {"src": None, "gath": None, "n": 0}
        B_xs, B_xg = Buf("xsrc"), Buf("xgath")
    dbg_d = {}
    if dbg:
        for name, shape in dbg.items():
            dbg_d[name] = dout(name, shape)

    AR = Arena(nc, 206 * 1024)
    ps = nc.alloc_psum_tensor("ps", [128, 4096], F32)
    PSB = [Buf("ps%d" % i) for i in range(8)]
    psn = [0]

    def nb():
        b = psn[0] % 8
        psn[0] += 1
        return b

    def bank(b):
        return ps[:, b * 512:(b + 1) * 512]

    def mm(out, lhsT, rhs, start, stop, r, w):
        return S.add("pe", lambda e: e.matmul(out, lhsT=lhsT, rhs=rhs, start=start, stop=stop), r=r, w=w)

    def tr(out, in_, ident, r, w):
        return S.add("pe", lambda e: e.transpose(out, in_, ident), r=r, w=w)

    def act(out, in_, func, r, w, scale=None, bias=None, accum=None):
        kw = {}
        if scale is not None:
            kw["scale"] = scale
        if bias is not None:
            kw["bias"] = bias
        if accum is not None:
            kw["accum_out"] = accum
        return S.add("act", lambda e: e.activation(out=out, in_=in_, func=func, **kw), r=r, w=w)

    def tt(eng, out, in0, in1, op, r, w):
        return S.add(eng, lambda e: e.tensor_tensor(out=out, in0=in0, in1=in1, op=op), r=r, w=w)

    def ts(eng, out, in0, s1, s2, op0, op1, r, w):
        if op1 is None:
            return S.add(eng, lambda e: e.tensor_scalar(out=out, in0=in0, scalar1=s1, scalar2=None, op0=op0),
                         r=r, w=w)
        return S.add(eng, lambda e: e.tensor_scalar(out=out, in0=in0, scalar1=s1, scalar2=s2, op0=op0, op1=op1),
                     r=r, w=w)

    def stt(out, in0, scalar, in1, op0, op1, r, w):
        return S.add("dve", lambda e: e.scalar_tensor_tensor(out=out, in0=in0, scalar=scalar, in1=in1,
                                                             op0=op0, op1=op1), r=r, w=w)

    def cp(eng, out, in_, r, w):
        return S.add(eng, lambda e: e.tensor_copy(out=out, in_=in_), r=r, w=w)

    def dma(eng, out, in_, r, w):
        return S.add(eng, lambda e: e.dma_start(out=out, in_=in_), r=r, w=w, dma=True)

    def newton_rsqrt(y, a, t, By, Ba, Bt):
        yi = y.bitcast(I32)
        ai = a.bitcast(I32)
        ts("dve", yi, ai, 1, None, ALU.arith_shift_right, None, r=[Ba], w=[By])
        ts("dve", yi, yi, -1, 0x5F3759DF, ALU.mult, ALU.add, r=[By], w=[By])
        for _ in range(3):
            tt("dve", t, y, y, ALU.mult, r=[By], w=[Bt])
            tt("dve", t, t, a, ALU.mult, r=[Bt, Ba], w=[Bt])
            ts("dve", t, t, -0.5, 1.5, ALU.mult, ALU.add, r=[Bt], w=[Bt])
            tt("dve", y, y, t, ALU.mult, r=[By, Bt], w=[By])

    def sqrt_recip(r_, a_, Br, Ba):
        act(a_, a_, AF.Sqrt, r=[Ba], w=[Ba])
        S.add("dve", lambda e: e.reciprocal(out=r_, in_=a_), r=[Ba], w=[Br])

    onesb = AR.view([128, 128], BF16)
    identb = AR.view([128, 128], BF16)
    identf = AR.view([128, 128], F32)
    cact = AR.view([128, KT], F32)
    modT = AR.view([128, DEPTH, 48], F32)
    gains = AR.view([128, DEPTH, 32], F32)
    prm = AR.view([128, DEPTH, 48], F32)
    ring = [AR.view([128, UNIT], BF16) for _ in range(RING_SLOTS)]
    RB = [Buf("ring%d" % i) for i in range(RING_SLOTS)]
    eps6 = AR.view([128, 1], F32)
    B_const = Buf("const")
    B_prm = Buf("prm")
    persist_mark = AR.mark()

    wseq = []
    wstate = {"issued": 0, "n": 0}
    PF = 3

    def wnext(tag):
        if S.dry:
            wseq.append((wbd[tag[0]][tag[2]], tag))
            n = len(wseq) - 1
            return ring[n % RING_SLOTS], RB[n % RING_SLOTS]
        while wstate["issued"] < min(len(wseq), wstate["n"] + PF + 1):
            i = wstate["issued"]
            dma("pool", ring[i % RING_SLOTS], wseq[i][0], r=[B_wb[wseq[i][1][0]][wseq[i][1][2]]],
                w=[RB[i % RING_SLOTS]])
            wstate["issued"] += 1
            if pend_pc:
                precast_unit(*pend_pc.pop(0))
        n = wstate["n"]
        assert wseq[n][1] == tag, (wseq[n][1], tag)
        wstate["n"] += 1
        return ring[n % RING_SLOTS], RB[n % RING_SLOTS]

    def precast_unit(pi, u):
        dma("pool", wbd[pi][u], wd[pi][u], r=[], w=[B_wb[pi][u]])

    def precast(pi, now=False):
        if pi < len(passes):
            for u in range(units_of(passes[pi][0])):
                if now:
                    precast_unit(pi, u)
                else:
                    pend_pc.append((pi, u))

    precast(0, now=True)
    S.add("pool", lambda e: e.memset(onesb, 1.0), w=[B_const])
    S.add("pool", lambda e: e.memset(eps6, 1e-6), w=[B_const])
    dma("sp", identf, ident_d, r=[], w=[B_const])
    cp("dve", identb, identf, r=[B_const], w=[B_const])
    dma("sp", cact, cT_d, r=[], w=[B_prm])
    act(cact, cact, AF.Silu, r=[B_prm], w=[B_prm])
    dma("sp", gains, gains_d.rearrange("l p k -> p l k"), r=[], w=[B_prm])
    m0 = AR.mark()
    stage = [AR.view([128, KT, 768], BF16) for _ in range(3)]
    cactb = AR.view([128, KT], BF16)
    cp("dve", cactb, cact, r=[B_prm], w=[B_prm])
    SB_ = [Buf("stage0"), Buf("stage1"), Buf("stage2")]
    adabT = AR.view([128, DEPTH, 48], F32)
    dma("sp", adabT, adab_d.rearrange("l p k -> p l k"), r=[], w=[B_prm])
    layers_needed = sorted(set(l for _, l in passes))
    si = 0
    for l in layers_needed:
        pb = nb()
        for blk in range(8):
            st_, sb_ = stage[si % 3], SB_[si % 3]
            si += 1
            dma("pool", st_, adaw_d[l, blk].rearrange("p (k c) -> p k c", k=KT), r=[], w=[sb_])
            for jj in range(6):
                j = blk * 6 + jj
                for kt in range(KT):
                    mm(bank(pb)[:, j:j + 1], st_[:, kt, jj * 128:(jj + 1) * 128], cactb[:, kt:kt + 1],
                       kt == 0, kt == KT - 1, r=[sb_, B_prm], w=[PSB[pb]])
        tt("dve", modT[:, l, :], bank(pb)[:, 0:48], adabT[:, l, :], ALU.add, r=[PSB[pb], B_prm], w=[B_prm])
        for half, (g_pre, g_post) in enumerate(((0, 1), (2, 3))):
            mo = half * 24
            po = half * 24
            stt(prm[:, l, po:po + 8], modT[:, l, mo + 8:mo + 16], 1.0, gains[:, l, g_pre * 8:g_pre * 8 + 8],
                ALU.add, ALU.mult, r=[B_prm], w=[B_prm])
            cp("dve", prm[:, l, po + 8:po + 16], modT[:, l, mo:mo + 8], r=[B_prm], w=[B_prm])
            tt("dve", prm[:, l, po + 16:po + 24], modT[:, l, mo + 16:mo + 24],
               gains[:, l, g_post * 8:g_post * 8 + 8], ALU.mult, r=[B_prm], w=[B_prm])
    if dbg and "prm" in dbg:
        dma("sp", dbg_d["prm"], prm.rearrange("p l k -> p (l k)"), r=[B_prm], w=[])
    S.barrier()
    AR.reset(m0)

    XB = [Buf("x%d" % i) for i in range(ntok // 512)]
    state = {"first": first_from_input}
    final_dmas = []

    def make_common(TT, nh=1):
        c = {}
        c["xin"] = AR.view([128, KT, 512], F32)
        c["sq"] = AR.view([128, KT, 512], BF16)
        c["hTs"] = [AR.view([128, KT, TT], BF16) for _ in range(nh)]
        c["hT"] = c["hTs"][0]
        c["a"] = AR.view([128, 512], F32)
        c["rs"] = AR.view([128, 512], F32)
        c["t"] = AR.view([128, 512], F32)
        c["yT"] = AR.view([128, KT, TT], F32)
        c["xres"] = [AR.view([128, 512], F32) for _ in range(2)]
        for n in ("xin", "sq", "a", "rs", "t"):
            c["B_" + n] = Buf(n)
        c["B_hTs"] = [Buf("hT%d" % i) for i in range(nh)]
        c["B_hT"] = c["B_hTs"][0]
        c["B_yT"] = [[Buf("yT") for _ in range(TT // 512)] for _ in range(KT)]
        c["B_xres"] = [Buf("xres0"), Buf("xres1")]
        c["B_sqk"] = [Buf("sq%d" % i) for i in range(KT)]
        c["xr"] = 0
        c["pend"] = []
        return c

    epsT = AR_eps = None

    def rstd_from_bank(c, b, eps):
        act(c["a"], bank(b), AF.Sqrt, r=[PSB[b], B_const], w=[c["B_a"]], scale=1.0 / D, bias=eps6[:, 0:1])
        S.add("dve", lambda e: e.reciprocal(out=c["rs"], in_=c["a"]), r=[c["B_a"]], w=[c["B_rs"]])

    def rms_stats(c, src3, Bsrc_list, eps):
        b = nb()
        for kt in range(KT):
            act(c["sq"][:, kt, :], src3[:, kt, :], AF.Square, r=[Bsrc_list[kt]], w=[c["B_sqk"][kt]])
            mm(bank(b), onesb, c["sq"][:, kt, :], kt == 0, kt == KT - 1, r=[c["B_sqk"][kt], B_const], w=[PSB[b]])
        rstd_from_bank(c, b, eps)

    def pre_norm(c, src_d, tok0, hoff, gs, sh):
        xin = c["xin"]
        dma("sp", xin, src_d.rearrange("(k p) t -> p k t", p=128)[:, :, tok0:tok0 + 512],
            r=[XB[tok0 // 512]], w=[c["B_xin"]])
        rms_stats(c, xin, [c["B_xin"]] * KT, 1e-6)
        for kt in range(KT):
            tt("dve", xin[:, kt, :], xin[:, kt, :], c["rs"], ALU.mult, r=[c["B_xin"], c["B_rs"]], w=[c["B_xin"]])
        for kt in range(KT):
            act(c["hT"][:, kt, hoff:hoff + 512], xin[:, kt, :], AF.Identity, r=[c["B_xin"], B_prm], w=[c["B_hT"]],
                scale=gs[:, kt:kt + 1], bias=sh[:, kt:kt + 1])

    def evac_y(c, b, dt, hf, gg, sbank):
        sl = slice(hf * 512, (hf + 1) * 512)
        act(c["sq"][:, dt, :], bank(b), AF.Square, r=[PSB[b]], w=[c["B_sqk"][dt]])
        act(c["yT"][:, dt, sl], bank(b), AF.Copy, r=[PSB[b], B_prm], w=[c["B_yT"][dt][hf]], scale=gg[:, dt:dt + 1])
        c["pend"].append((dt, sbank))

    def flush_pend(c, keep=0):
        while len(c["pend"]) > keep:
            dt, sbank = c["pend"].pop(0)
            mm(bank(sbank), onesb, c["sq"][:, dt, :], dt == 0, dt == KT - 1, r=[c["B_sqk"][dt], B_const],
               w=[PSB[sbank]])

    def post_norm_store(c, src_d, dst_d, tok0, hf, sbank):
        flush_pend(c)
        yh = c["yT"][:, :, hf * 512:(hf + 1) * 512]
        rstd_from_bank(c, sbank, 1e-6)
        for kt in range(KT):
            k = c["xr"] % 2
            c["xr"] += 1
            dma("sp", c["xres"][k], src_d[kt * 128:(kt + 1) * 128, tok0:tok0 + 512],
                r=[XB[tok0 // 512]], w=[c["B_xres"][k]])
            tt("dve", yh[:, kt, :], yh[:, kt, :], c["rs"], ALU.mult,
               r=[c["B_yT"][kt][hf], c["B_rs"]], w=[c["B_yT"][kt][hf]])
            tt("dve", yh[:, kt, :], yh[:, kt, :], c["xres"][k], ALU.add,
               r=[c["B_yT"][kt][hf], c["B_xres"][k]], w=[c["B_yT"][kt][hf]])
        o = dma("sp", dst_d.rearrange("(k p) t -> p k t", p=128)[:, :, tok0:tok0 + 512], yh,
                r=[c["B_yT"][kt][hf] for kt in range(KT)], w=[XB[tok0 // 512]])
        final_dmas.append(o)

    def pass_F(pi, l):
        TT = TT_OF["F"]
        NH = TT // 512
        src = xT_d if state["first"] else outT_d
        m = AR.mark()
        c = make_common(TT)
        aT = AR.view([128, 32, TT], BF16)
        B_aT = [Buf("aT%d" % i) for i in range(32)]
        rt = [AR.view([128, 512], F32) for _ in range(2)]
        B_rt = [Buf("rt0"), Buf("rt1")]
        gs, sh, gg = prm[:, l, 24:32], prm[:, l, 32:40], prm[:, l, 40:48]
        nt = ntok // TT
        rk = [0]

        def pre(ti):
            for hf in range(NH):
                pre_norm(c, src, ti * TT + hf * 512, hf * 512, gs, sh)

        def inproj(ti):
            for u in range(8):
                wap, wb = wnext((pi, ti, u))
                w3 = wap.rearrange("p (k c) -> p k c", k=KT)
                for f in range(4):
                    ft = u * 4 + f
                    for hf in range(NH):
                        b = nb()
                        for kt in range(KT):
                            mm(bank(b), w3[:, kt, f * 128:(f + 1) * 128], c["hT"][:, kt, hf * 512:(hf + 1) * 512],
                               kt == 0, kt == KT - 1, r=[wb, c["B_hT"]], w=[PSB[b]])
                        k = rk[0] % 2
                        rk[0] += 1
                        act(rt[k], bank(b), AF.Relu, r=[PSB[b]], w=[B_rt[k]])
                        act(aT[:, ft, hf * 512:(hf + 1) * 512], rt[k], AF.Square, r=[B_rt[k]], w=[B_aT[ft]])

        def outproj(ti):
            sb = [nb() for _ in range(NH)]
            for dt in range(KT):
                wap, wb = wnext((pi, ti, 8 + dt))
                w3 = wap.rearrange("p (j c) -> p j c", j=32)
                for hf in range(NH):
                    b = nb()
                    while b in sb:
                        b = nb()
                    for j in range(32):
                        mm(bank(b), w3[:, j, :], aT[:, j, hf * 512:(hf + 1) * 512], j == 0, j == 31,
                           r=[wb, B_aT[j]], w=[PSB[b]])
                    flush_pend(c)
                    evac_y(c, b, dt, hf, gg, sb[hf])
            for hf in range(NH):
                post_norm_store(c, src, outT_d, ti * TT + hf * 512, hf, sb[hf])

        pre(0)
        for ti in range(nt):
            inproj(ti)
            if ti + 1 < nt:
                pre(ti + 1)
            outproj(ti)
        S.barrier()
        AR.reset(m)
        state["first"] = False


    def pass_A(pi, l):
        TT = TT_OF["A"]
        assert TT == 512
        jA = l // 2
        src = xT_d if state["first"] else outT_d
        m = AR.mark()
        c = make_common(TT, nh=2)

        def sel_h(ti):
            c["hT"] = c["hTs"][ti % 2]
            c["B_hT"] = c["B_hTs"][ti % 2]

        uT = AR.view([128, 16, 512], BF16)
        B_uT = [Buf("uT%d" % i) for i in range(16)]
        vgbs = [AR.view([128, 4, 2048], BF16) for _ in range(2)]
        B_vgbs = [[Buf("vgb%d_%d" % (p_, i)) for i in range(4)] for p_ in range(2)]
        gt = [AR.view([128, 512], F32) for _ in range(3)]
        B_gt = [Buf("gt%d" % i) for i in range(3)]
        vn = [AR.view([128, 2048], BF16) for _ in range(2)]
        B_vn = [Buf("vn0"), Buf("vn1")]
        sT = AR.view([128, 16, 512], BF16)
        B_sT = [Buf("sT%d" % i) for i in range(4)]
        lngT = AR.view([128, 16], F32)
        lnbT = AR.view([128, 16], F32)
        bsj = AR.view([128, 16, 128], F32)
        WsT = AR.view([128, 8, 128], BF16)
        wtmp = AR.view([128, 8, 128], F32)
        maskT = AR.view([128, 128], F32)
        sts = [AR.view([128, 4, 24], F32) for _ in range(2)]
        mvs = [AR.view([128, 4, 2], F32) for _ in range(2)]
        a4s = [AR.view([128, 4], F32) for _ in range(2)]
        r4s = [AR.view([128, 4], F32) for _ in range(2)]
        n4s = [AR.view([128, 4], F32) for _ in range(2)]
        B_cA = Buf("constA")
        B_sts = [Buf("st0"), Buf("st1")]
        B_mvs = [Buf("mv0"), Buf("mv1")]
        B_a4s = [Buf("a40"), Buf("a41")]
        B_r4s = [Buf("r40"), Buf("r41")]
        B_n4s = [Buf("n40"), Buf("n41")]
        gs, sh, gg = prm[:, l, 0:8], prm[:, l, 8:16], prm[:, l, 16:24]
        dma("sp", lngT, lngT_d[jA], r=[], w=[B_cA])
        dma("sp", lnbT, lnbT_d[jA], r=[], w=[B_cA])
        dma("sp", bsj, bsj_d[jA].rearrange("p (j t) -> p j t", j=16), r=[], w=[B_cA])
        dma("sp", wtmp, wst_d[jA].rearrange("p (g t) -> p g t", g=8), r=[], w=[B_cA])
        dma("sp", maskT, maskT_d, r=[], w=[B_cA])
        for g in range(8):
            tt("dve", WsT[:, g, :], wtmp[:, g, :], maskT, ALU.mult, r=[B_cA], w=[B_cA])
        for gb in range(2):
            b = nb()
            for g4 in range(4):
                g = gb * 4 + g4
                mm(bank(b)[:, g4 * 128:(g4 + 1) * 128], onesb, WsT[:, g, :], True, True, r=[B_cA, B_const],
                   w=[PSB[b]])
            for g4 in range(4):
                g = gb * 4 + g4
                for j in (2 * g, 2 * g + 1):
                    stt(bsj[:, j, :], bank(b)[:, g4 * 128:(g4 + 1) * 128], lnbT[:, j:j + 1], bsj[:, j, :],
                        ALU.mult, ALU.add, r=[PSB[b], B_cA], w=[B_cA])
        nt = ntok // TT
        ks = {"gt": 0, "t1": 0, "se": 0}

        def uproj_unit(ti, u):
            wap, wb = wnext((pi, ti, u))
            w3 = wap.rearrange("p (k c) -> p k c", k=KT)
            for f in range(4):
                ft = u * 4 + f
                b = nb()
                for kt in range(KT):
                    mm(bank(b), w3[:, kt, f * 128:(f + 1) * 128], c["hT"][:, kt, :], kt == 0, kt == KT - 1,
                       r=[wb, c["B_hT"]], w=[PSB[b]])
                act(uT[:, ft, :], bank(b), AF.Gelu_apprx_tanh, r=[PSB[b]], w=[B_uT[ft]])

        def vproj(ti):
            pp = ti % 2
            vgb, B_vgb = vgbs[pp], B_vgbs[pp]
            st, mv, a4, r4, n4 = sts[pp], mvs[pp], a4s[pp], r4s[pp], n4s[pp]
            B_st, B_mv, B_a4, B_r4, B_n4 = B_sts[pp], B_mvs[pp], B_a4s[pp], B_r4s[pp], B_n4s[pp]
            for u in range(4):
                wap, wb = wnext((pi, ti, 4 + u))
                w3 = wap.rearrange("p (k c) -> p k c", k=KT)
                for ch in range(4):
                    b = nb()
                    for kt in range(KT):
                        mm(bank(b), c["hT"][:, kt, ch * 128:(ch + 1) * 128], w3[:, kt, :], kt == 0, kt == KT - 1,
                           r=[wb, c["B_hT"]], w=[PSB[b]])
                    k = ks["gt"] % 3
                    ks["gt"] += 1
                    act(gt[k], bank(b), AF.Gelu_apprx_tanh, r=[PSB[b]], w=[B_gt[k]])
                    S.add("dve", lambda e, o=st[:, ch, u * 6:(u + 1) * 6], i=gt[k]: e.bn_stats(out=o, in_=i),
                          r=[B_gt[k]], w=[B_st])
                    act(vgb[:, ch, u * 512:(u + 1) * 512], gt[k], AF.Copy, r=[B_gt[k]], w=[B_vgb[ch]])
            for ch in range(4):
                S.add("dve", lambda e, o=mv[:, ch, :], i=st[:, ch, :]: e.bn_aggr(out=o, in_=i), r=[B_st], w=[B_mv])
            ts("dve", a4, mv[:, :, 1], 1e-5, None, ALU.add, None, r=[B_mv], w=[B_a4])
            sqrt_recip(r4, a4, B_r4, B_a4)
            stt(n4, mv[:, :, 0], -1.0, r4, ALU.mult, ALU.mult, r=[B_mv, B_r4], w=[B_n4])

        def spatial_chunk(ti, ch):
            pp = ti % 2
            vgb, B_vgb = vgbs[pp], B_vgbs[pp]
            r4, n4, B_r4, B_n4 = r4s[pp], n4s[pp], B_r4s[pp], B_n4s[pp]
            k = ks["t1"] % 2
            ks["t1"] += 1
            for hh in range(2):
                act(vn[k][:, hh * 1024:(hh + 1) * 1024], vgb[:, ch, hh * 1024:(hh + 1) * 1024], AF.Identity,
                    r=[B_vgb[ch], B_r4, B_n4], w=[B_vn[k]], scale=r4[:, ch:ch + 1], bias=n4[:, ch:ch + 1])
            for jb in range(4):
                b = nb()
                for jj in range(4):
                    j = jb * 4 + jj
                    mm(bank(b)[:, jj * 128:(jj + 1) * 128], vn[k][:, j * 128:(j + 1) * 128], WsT[:, j // 2, :],
                       True, True, r=[B_vn[k], B_cA], w=[PSB[b]])
                for jj in range(4):
                    j = jb * 4 + jj
                    stt(sT[:, j, ch * 128:(ch + 1) * 128], bank(b)[:, jj * 128:(jj + 1) * 128], lngT[:, j:j + 1],
                        bsj[:, j, :], ALU.mult, ALU.add, r=[PSB[b], B_cA], w=[B_sT[ch]])

        def gate(ti):
            for jb in range(4):
                uv = uT[:, jb * 4:(jb + 1) * 4, :]
                tt("dve", uv, uv, sT[:, jb * 4:(jb + 1) * 4, :], ALU.mult, r=B_sT + B_uT[jb * 4:(jb + 1) * 4],
                   w=B_uT[jb * 4:(jb + 1) * 4])

        def outproj(ti):
            sbank = nb()
            for u in range(4):
                wap, wb = wnext((pi, ti, 8 + u))
                w4 = wap.rearrange("p (g j c) -> p g j c", g=2, j=16)
                for g2 in range(2):
                    dt = u * 2 + g2
                    b = nb()
                    if b == sbank:
                        b = nb()
                    for j in range(16):
                        mm(bank(b), w4[:, g2, j, :], uT[:, j, :], j == 0, j == 15, r=[wb, B_uT[j]], w=[PSB[b]])
                    flush_pend(c)
                    evac_y(c, b, dt, 0, gg, sbank)
            post_norm_store(c, src, outT_d, ti * TT, 0, sbank)

        sel_h(0)
        pre_norm(c, src, 0, 0, gs, sh)
        vproj(0)
        for ti in range(nt):
            more = ti + 1 < nt
            if more:
                sel_h(ti + 1)
                pre_norm(c, src, (ti + 1) * TT, 0, gs, sh)
            sel_h(ti)
            uproj_unit(ti, 0)
            uproj_unit(ti, 1)
            spatial_chunk(ti, 0)
            uproj_unit(ti, 2)
            spatial_chunk(ti, 1)
            uproj_unit(ti, 3)
            spatial_chunk(ti, 2)
            if more:
                sel_h(ti + 1)
                vproj(ti + 1)
                sel_h(ti)
            spatial_chunk(ti, 3)
            gate(ti)
            outproj(ti)
        S.barrier()
        AR.reset(m)
        state["first"] = False

    def pass_B(pi, l, sweep=False):
        TT = 512
        jB = l // 2
        src = xT_d if state["first"] else outT_d
        m = AR.mark()
        c = make_common(TT)
        csb = [[AR.view([128, 512], F32) for _ in range(2)] for _ in range(2)]
        B_csb = [Buf("cs0"), Buf("cs1")]
        ra = [AR.view([128, 512], F32) for _ in range(2)]
        rb = [AR.view([128, 512], F32) for _ in range(2)]
        B_ra = [Buf("ra0"), Buf("ra1")]
        B_rb = [Buf("rb0"), Buf("rb1")]
        kT = AR.view([128, 8, 512], BF16)
        B_kT = [Buf("kT%d" % i) for i in range(8)]
        v = AR.view([128, 4, 2048], BF16)
        B_v = [Buf("v%d" % i) for i in range(4)]
        kvs = [{"kT": kT, "B_kT": B_kT, "v": v, "B_v": B_v}]
        if sweep:
            kvs.append({"kT": AR.view([128, 8, 512], BF16), "B_kT": [Buf("kTb%d" % i) for i in range(8)],
                        "v": AR.view([128, 4, 2048], BF16), "B_v": [Buf("vb%d" % i) for i in range(4)]})
        cur = dict(kvs[0])

        def select(ti):
            cur.update(kvs[ti % len(kvs)])
        kd = [AR.view([128, 1024], BF16) for _ in range(2)]
        B_kd = [Buf("kd0"), Buf("kd1")]
        stf = AR.view([128, 8, 512], F32)
        B_stf = [Buf("stf%d" % i) for i in range(8)]
        dk = AR.view([128, H], F32)
        B_cB = Buf("constB")
        dma("sp", dk, dk_d, r=[], w=[B_cB])
        if not sweep:
            qT = AR.view([128, 8, 512], BF16)
            B_qT = [Buf("qT%d" % i) for i in range(8)]
            sgT = AR.view([128, 16, 512], BF16)
            B_sg = [Buf("sg%d" % i) for i in range(16)]
            scT = [AR.view([128, 512], BF16) for _ in range(2)]
            B_sc = [Buf("sc0"), Buf("sc1")]
            on = [AR.view([128, 2048], BF16) for _ in range(2)]
            B_on = [Buf("on0"), Buf("on1")]
            ot = [AR.view([128, 8, 128], BF16) for _ in range(2)]
            B_ot = [Buf("ot0"), Buf("ot1")]
            stb = AR.view([128, 8, 512], BF16)
            B_stb = [Buf("stb%d" % i) for i in range(8)]
            dmask = AR.view([128, 512], F32)
            epsp = AR.view([128, H], F32)
            gng = AR.view([128, 16], F32)
            gnb = AR.view([128, 16], F32)
            flag = AR.view([128, 1], F32)
            gst = AR.view([128, 4, H, 6], F32)
            gmv = AR.view([128, 4, H, 2], F32)
            a4 = AR.view([128, 4, H], F32)
            r4 = AR.view([128, 4, H], F32)
            n4 = AR.view([128, 4, H], F32)
            B_gst = [Buf("gst%d" % i) for i in range(4)]
            B_gmv = [Buf("gmv%d" % i) for i in range(4)]
            B_a4 = [Buf("a4%d" % i) for i in range(4)]
            B_r4 = [Buf("r4%d" % i) for i in range(4)]
            B_n4 = [Buf("n4%d" % i) for i in range(4)]
            dma("sp", dmask, dmask_d, r=[], w=[B_cB])
            dma("sp", epsp, epsp_d, r=[], w=[B_cB])
            dma("sp", gng, gng_d[jB], r=[], w=[B_cB])
            dma("sp", gnb, gnb_d[jB], r=[], w=[B_cB])
            dma("sp", flag, flag_d, r=[], w=[B_cB])
            if fused and S.dry:
                pass
            elif fused:
                dma("sp", stf.rearrange("p a b -> p (a b)"), xch["gath"][0:128, :], r=[B_xg], w=B_stf)
            else:
                dma("sp", stf.rearrange("p a b -> p (a b)"), st_in_d, r=[], w=B_stf)
            for i in range(8):
                ts("dve", stf[:, i, :], stf[:, i, :], flag[:, 0:1], None, ALU.mult, None, r=[B_stf[i], B_cB],
                   w=[B_stf[i]])
                cp("pool", stb[:, i, :], stf[:, i, :], r=[B_stf[i]], w=[B_stb[i]])
        else:
            S.add("pool", lambda e: e.memset(stf.rearrange("p a b -> p (a b)"), 0.0), w=B_stf)
        gs, sh, gg = prm[:, l, 0:8], prm[:, l, 8:16], prm[:, l, 16:24]
        nt = ntok // TT
        ks = {"r": 0, "kd": 0, "sc": 0, "on": 0, "ot": 0}
        GC = [GAMMA[h] ** CH for h in range(H)]

        def rot_unit(ti, u_idx, dst, B_dst, hbase):
            wap, wb = wnext((pi, ti, u_idx))
            w3 = wap.rearrange("p (k c) -> p k c", k=KT)
            for hh in range(2):
                h = hbase + hh
                b1, b2 = nb(), nb()
                for bb, f in ((b1, hh * 2), (b2, hh * 2 + 1)):
                    for kt in range(KT):
                        mm(bank(bb), w3[:, kt, f * 128:(f + 1) * 128], c["hT"][:, kt, :], kt == 0, kt == KT - 1,
                           r=[wb, c["B_hT"]], w=[PSB[bb]])
                k = ks["r"] % 2
                ks["r"] += 1
                cs, B_cs = csb[ti % 2], B_csb[ti % 2]
                tt("dve", ra[k], bank(b1), cs[0], ALU.mult, r=[PSB[b1], B_cs], w=[B_ra[k]])
                tt("dve", rb[k], bank(b2), cs[1], ALU.mult, r=[PSB[b2], B_cs], w=[B_rb[k]])
                tt("pool", dst[:, 2 * h, :], ra[k], rb[k], ALU.subtract, r=[B_ra[k], B_rb[k]], w=[B_dst[2 * h]])
                k = ks["r"] % 2
                ks["r"] += 1
                tt("dve", ra[k], bank(b1), cs[1], ALU.mult, r=[PSB[b1], B_cs], w=[B_ra[k]])
                tt("dve", rb[k], bank(b2), cs[0], ALU.mult, r=[PSB[b2], B_cs], w=[B_rb[k]])
                tt("pool", dst[:, 2 * h + 1, :], ra[k], rb[k], ALU.add, r=[B_ra[k], B_rb[k]], w=[B_dst[2 * h + 1]])

        def v_units(ti, u0):
            for u in range(4):
                wap, wb = wnext((pi, ti, u0 + u))
                w3 = wap.rearrange("p (k c) -> p k c", k=KT)
                for ch in range(4):
                    b = nb()
                    for kt in range(KT):
                        mm(bank(b), c["hT"][:, kt, ch * 128:(ch + 1) * 128], w3[:, kt, :], kt == 0, kt == KT - 1,
                           r=[wb, c["B_hT"]], w=[PSB[b]])
                    act(cur["v"][:, ch, u * 512:(u + 1) * 512], bank(b), AF.Copy, r=[PSB[b]],
                        w=[cur["B_v"][ch]])

        def k_transpose(ch):
            k = ks["kd"] % 2
            ks["kd"] += 1
            b = nb()
            bb = bank(b).bitcast(BF16)
            for ft in range(8):
                tr(bb[:, ft * 128:(ft + 1) * 128], cur["kT"][:, ft, ch * 128:(ch + 1) * 128], identb,
                   r=[cur["B_kT"][ft], B_const], w=[PSB[b]])
            for h in range(H):
                act(kd[k][:, h * 256:(h + 1) * 256], bb[:, h * 256:(h + 1) * 256], AF.Copy, r=[PSB[b], B_cB],
                    w=[B_kd[k]], scale=dk[:, h:h + 1])
            return k

        def state_update(ch, k):
            for h in range(H):
                for d2 in range(2):
                    i = 2 * h + d2
                    b = nb()
                    mm(bank(b), kd[k][:, h * 256 + d2 * 128:h * 256 + (d2 + 1) * 128],
                       cur["v"][:, ch, h * 512:(h + 1) * 512], True, True, r=[B_kd[k], cur["B_v"][ch]], w=[PSB[b]])
                    stt(stf[:, i, :], stf[:, i, :], float(GC[h]), bank(b), ALU.mult, ALU.add,
                        r=[B_stf[i], PSB[b]], w=[B_stf[i]])
                    if not sweep:
                        act(stb[:, i, :], stf[:, i, :], AF.Copy, r=[B_stf[i]], w=[B_stb[i]])

        def load_cs(ti):
            cs, B_cs = csb[ti % 2], B_csb[ti % 2]
            dma("sp", cs[0], cos_d[:, ti * TT:(ti + 1) * TT], r=[], w=[B_cs])
            dma("sp", cs[1], sin_d[:, ti * TT:(ti + 1) * TT], r=[], w=[B_cs])

        def proj(ti):
            if sweep:
                rot_unit(ti, 0, cur["kT"], cur["B_kT"], 0)
                rot_unit(ti, 1, cur["kT"], cur["B_kT"], 2)
                v_units(ti, 2)
            else:
                rot_unit(ti, 0, qT, B_qT, 0)
                rot_unit(ti, 1, qT, B_qT, 2)
                rot_unit(ti, 2, kT, B_kT, 0)
                rot_unit(ti, 3, kT, B_kT, 2)
                v_units(ti, 4)
                for u in range(4):
                    wap, wb = wnext((pi, ti, 8 + u))
                    w3 = wap.rearrange("p (k c) -> p k c", k=KT)
                    for f in range(4):
                        ft = u * 4 + f
                        b = nb()
                        for kt in range(KT):
                            mm(bank(b), w3[:, kt, f * 128:(f + 1) * 128], c["hT"][:, kt, :], kt == 0, kt == KT - 1,
                               r=[wb, c["B_hT"]], w=[PSB[b]])
                        act(sgT[:, ft, :], bank(b), AF.Silu, r=[PSB[b]], w=[B_sg[ft]])

        def chunks(ti):
            if sweep:
                k0 = k_transpose(0)
                k1 = k_transpose(1)
                state_update(0, k0)
                k2 = k_transpose(2)
                state_update(1, k1)
                k3 = k_transpose(3)
                state_update(2, k2)
                state_update(3, k3)
                return
            kk = {}
            sck = {}
            onk = {}

            def stage_a(ch):
                tok = slice(ch * 128, (ch + 1) * 128)
                kk[ch] = k_transpose(ch)
                b = nb()
                for h in range(H):
                    for d2 in range(2):
                        i = 2 * h + d2
                        mm(bank(b)[:, h * 128:(h + 1) * 128], kT[:, i, tok], qT[:, i, tok], d2 == 0, d2 == 1,
                           r=[B_kT[i], B_qT[i]], w=[PSB[b]])
                k2 = ks["sc"] % 2
                ks["sc"] += 1
                sck[ch] = k2
                tt("dve", scT[k2], bank(b), dmask, ALU.mult, r=[PSB[b], B_cB], w=[B_sc[k2]])

            def stage_b(ch):
                tok = slice(ch * 128, (ch + 1) * 128)
                k2 = sck[ch]
                k3 = ks["on"] % 2
                ks["on"] += 1
                onk[ch] = k3
                for h in range(H):
                    b = nb()
                    mm(bank(b), scT[k2][:, h * 128:(h + 1) * 128], v[:, ch, h * 512:(h + 1) * 512], True, False,
                       r=[B_sc[k2], B_v[ch]], w=[PSB[b]])
                    for d2 in range(2):
                        i = 2 * h + d2
                        mm(bank(b), qT[:, i, tok], stb[:, i, :], False, d2 == 1, r=[B_qT[i], B_stb[i]],
                           w=[PSB[b]])
                    S.add("dve", lambda e, o=gst[:, ch, h, :], i_=bank(b): e.bn_stats(out=o, in_=i_),
                          r=[PSB[b]], w=[B_gst[ch]])
                    act(on[k3][:, h * 512:(h + 1) * 512], bank(b), AF.Copy, r=[PSB[b]], w=[B_on[k3]])
                state_update(ch, kk[ch])

            def stage_c(ch):
                tok = slice(ch * 128, (ch + 1) * 128)
                k3 = onk[ch]
                for h in range(H):
                    S.add("dve", lambda e, o=gmv[:, ch, h, :], i_=gst[:, ch, h, :]: e.bn_aggr(out=o, in_=i_),
                          r=[B_gst[ch]], w=[B_gmv[ch]])
                tt("dve", a4[:, ch, :], gmv[:, ch, :, 1], epsp, ALU.add, r=[B_gmv[ch], B_cB], w=[B_a4[ch]])
                sqrt_recip(r4[:, ch, :], a4[:, ch, :], B_r4[ch], B_a4[ch])
                stt(n4[:, ch, :], gmv[:, ch, :, 0], -1.0, r4[:, ch, :], ALU.mult, ALU.mult,
                    r=[B_gmv[ch], B_r4[ch]], w=[B_n4[ch]])
                for h in range(H):
                    act(on[k3][:, h * 512:(h + 1) * 512], on[k3][:, h * 512:(h + 1) * 512], AF.Identity,
                        r=[B_on[k3], B_r4[ch], B_n4[ch]], w=[B_on[k3]], scale=r4[:, ch, h:h + 1],
                        bias=n4[:, ch, h:h + 1])
                for eb in range(2):
                    b = nb()
                    bb = bank(b).bitcast(BF16)
                    for e8 in range(8):
                        et = eb * 8 + e8
                        tr(bb[:, e8 * 128:(e8 + 1) * 128], on[k3][:, et * 128:(et + 1) * 128], identb,
                           r=[B_on[k3], B_const], w=[PSB[b]])
                    k4 = ks["ot"] % 2
                    ks["ot"] += 1
                    for e8 in range(8):
                        et = eb * 8 + e8
                        act(ot[k4][:, e8, :], bb[:, e8 * 128:(e8 + 1) * 128], AF.Identity, r=[PSB[b], B_cB],
                            w=[B_ot[k4]], scale=gng[:, et:et + 1], bias=gnb[:, et:et + 1])
                    sv = sgT[:, eb * 8:(eb + 1) * 8, tok]
                    tt("dve", sv, ot[k4], sv, ALU.mult, r=[B_ot[k4]] + B_sg[eb * 8:(eb + 1) * 8],
                       w=B_sg[eb * 8:(eb + 1) * 8])

            stage_a(0)
            stage_a(1)
            stage_b(0)
            stage_a(2)
            stage_b(1)
            stage_c(0)
            stage_a(3)
            stage_b(2)
            stage_c(1)
            stage_b(3)
            stage_c(2)
            stage_c(3)

        def outproj(ti):
            sbank = nb()
            for u in range(4):
                wap, wb = wnext((pi, ti, 12 + u))
                w4 = wap.rearrange("p (g j c) -> p g j c", g=2, j=16)
                for g2 in range(2):
                    dt = u * 2 + g2
                    b = nb()
                    if b == sbank:
                        b = nb()
                    for j in range(16):
                        mm(bank(b), w4[:, g2, j, :], sgT[:, j, :], j == 0, j == 15, r=[wb, B_sg[j]], w=[PSB[b]])
                    flush_pend(c)
                    evac_y(c, b, dt, 0, gg, sbank)
            post_norm_store(c, src, outT_d, ti * TT, 0, sbank)

        load_cs(0)
        pre_norm(c, src, 0, 0, gs, sh)
        if sweep:
            select(0)
            proj(0)
            for ti in range(nt):
                if ti + 1 < nt:
                    load_cs(ti + 1)
                    pre_norm(c, src, (ti + 1) * TT, 0, gs, sh)
                    select(ti + 1)
                    proj(ti + 1)
                select(ti)
                chunks(ti)
        else:
            for ti in range(nt):
                proj(ti)
                if ti + 1 < nt:
                    load_cs(ti + 1)
                    pre_norm(c, src, (ti + 1) * TT, 0, gs, sh)
                chunks(ti)
                outproj(ti)
        if sweep and fused and S.dry:
            pass
        elif sweep and fused:
            xch["n"] += 1
            xch["src"] = nc.dram_tensor("xsrc%d" % xch["n"], [128, 4096], F32, kind="Internal").ap()
            xch["gath"] = nc.dram_tensor("xgath%d" % xch["n"], [256, 4096], F32, kind="Internal").ap()
            dma("sp", xch["src"], stf.rearrange("p a b -> p (a b)"), r=B_stf, w=[B_xs])
            xs_, xg_ = xch["src"], xch["gath"]
            o = S.add("pool", lambda e: e.collective_compute(
                "AllGather", ALU.bypass, replica_groups=[[0, 1], [2, 3], [4, 5], [6, 7]],
                ins=[xs_[:, :]], outs=[xg_[:, :]]), r=[B_xs], w=[B_xg], dma=True)
            o.cc = True
        elif sweep:
            o = dma("sp", st_out_d, stf.rearrange("p a b -> p (a b)"), r=B_stf, w=[])
        S.barrier()
        AR.reset(m)
        if not sweep:
            state["first"] = False

    def pass_S(pi, l):
        pass_B(pi, l, sweep=True)

    PASS_FN = {"F": pass_F, "A": pass_A, "B": pass_B, "S": pass_S}
    S.dry = True
    sv_first, sv_psn = state["first"], psn[0]
    for pi, (k, l) in enumerate(passes):
        PASS_FN[k](pi, l)
    S.dry = False
    state["first"], psn[0] = sv_first, sv_psn
    del final_dmas[:]
    for pi, (k, l) in enumerate(passes):
        while pend_pc:
            precast_unit(*pend_pc.pop(0))
        precast(pi + 1)
        PASS_FN[k](pi, l)
    assert wstate["n"] == len(wseq), (wstate["n"], len(wseq))

    S.emit(stack, final_dmas[-(ntok // 512):] if final_dmas else [])
    stack.close()
    return nc


def tile_in(w):
    K, F_ = w.shape
    assert K == 1024
    return np.ascontiguousarray(w.reshape(8, 128, F_ // 512, 512).transpose(2, 1, 0, 3)).reshape(F_ // 512, 128, 4096)


def tile_out(w):
    K, N = w.shape
    assert N == 1024
    J = K // 128
    g = 32 // J
    t = w.reshape(J, 128, 8 // g, g, 128).transpose(2, 1, 3, 0, 4)
    return np.ascontiguousarray(t).reshape(8 // g, 128, 4096)


def pass_weights(kind, l, inp):
    j = l // 2
    if kind == "F":
        return np.concatenate([tile_in(inp["ffn_w_in"][l]), tile_out(inp["ffn_w_out"][l])], 0)
    if kind == "A":
        return np.concatenate([tile_in(inp["sg_w_in"][j]), tile_out(inp["sg_w_out"][j])], 0)
    if kind == "B":
        return np.concatenate([tile_in(inp["ret_w_in"][j]), tile_out(inp["ret_w_out"][j])], 0)
    if kind == "S":
        w = inp["ret_w_in"][j]
        return tile_in(np.ascontiguousarray(w[:, 1024:4096]))
    raise ValueError(kind)


def rope_tables():
    half = DK // 2
    inv_freq = (np.float32(10000.0) ** (-np.arange(half, dtype=np.float32) / np.float32(half))).astype(np.float32)
    pos = np.arange(SEQ, dtype=np.float32)
    ang = (pos[:, None] * inv_freq[None, :]).astype(np.float32)
    return np.cos(ang).astype(np.float32).T.copy(), np.sin(ang).astype(np.float32).T.copy()


def shared_inputs(inp, passes):
    f32 = np.float32
    sh = {}
    aw = np.asarray(inp["ada_w"], f32)
    sh["adaw"] = np.ascontiguousarray(aw.reshape(DEPTH, KT, 128, 8, 768).transpose(0, 3, 2, 1, 4)).reshape(
        DEPTH, 8, 128, KT * 768)
    sh["adab"] = np.ascontiguousarray(np.asarray(inp["ada_b"], f32).reshape(DEPTH, 48, 128).transpose(0, 2, 1))
    g = np.stack([np.asarray(inp[k], f32) for k in ("pre_mix_g", "post_mix_g", "pre_ffn_g", "post_ffn_g")], 1)
    sh["gains"] = np.ascontiguousarray(g.reshape(DEPTH, 4, KT, 128).transpose(0, 3, 1, 2)).reshape(DEPTH, 128, 32)
    sh["ident"] = np.eye(128, dtype=f32)
    inp32 = {k: np.asarray(v, f32) for k, v in inp.items()}
    for pi, (k, l) in enumerate(passes):
        sh["w%d" % pi] = pass_weights(k, l, inp32)
    kinds = [k for k, _ in passes]
    if "A" in kinds:
        ws = inp32["sg_w_s"]
        sh["wst"] = np.ascontiguousarray(ws.transpose(0, 3, 1, 2)).reshape(2, 128, 8 * 128)
        sh["maskT"] = np.triu(np.ones((128, 128), f32))
        sh["lngT"] = np.ascontiguousarray(inp32["sg_ln_g"].reshape(2, 16, 128).transpose(0, 2, 1))
        sh["lnbT"] = np.ascontiguousarray(inp32["sg_ln_b"].reshape(2, 16, 128).transpose(0, 2, 1))
        bs = inp32["sg_b_s"]
        bsj = np.repeat(bs, 2, axis=1)
        sh["bsj"] = np.ascontiguousarray(np.broadcast_to(bsj.reshape(2, 1, 16 * 128), (2, 128, 16 * 128)))
    if ("B" in kinds) or ("S" in kinds):
        g64 = np.array(GAMMA, np.float64)
        m_ = np.arange(128, dtype=np.float64)
        mask = (m_[:, None] <= m_[None, :]).astype(np.float64)
        dm = np.stack([mask * (g64[h] ** (-(m_ + 1.0)))[:, None] / 16.0 for h in range(H)], 1)
        sh["dmask"] = dm.reshape(128, H * 128).astype(f32)
        sh["epsp"] = np.stack([1e-5 / g64[h] ** (2.0 * (m_ + 1.0)) for h in range(H)], 1).astype(f32)
        sh["dk"] = np.stack([g64[h] ** (127.0 - m_) / 16.0 for h in range(H)], 1).astype(f32)
        sh["gng"] = np.ascontiguousarray(inp32["ret_gn_g"].reshape(2, 16, 128).transpose(0, 2, 1))
        sh["gnb"] = np.ascontiguousarray(inp32["ret_gn_b"].reshape(2, 16, 128).transpose(0, 2, 1))
    return sh


def core_inputs(inp, core, ntok, passes):
    f32 = np.float32
    b, half = core // 2, core % 2
    x = np.asarray(inp["x"], f32)
    t0 = half * TOK
    ci = {}
    ci["xT"] = np.ascontiguousarray(x[b, t0:t0 + ntok, :].T)
    ci["cT"] = np.ascontiguousarray(np.asarray(inp["c"], f32)[b].reshape(KT, 128).T)
    kinds = [k for k, _ in passes]
    if ("B" in kinds) or ("S" in kinds):
        cosT, sinT = rope_tables()
        ci["cosT"] = np.ascontiguousarray(cosT[:, t0:t0 + ntok])
        ci["sinT"] = np.ascontiguousarray(sinT[:, t0:t0 + ntok])
        ci["flag"] = np.full((128, 1), float(half), f32)
        ci["st_in"] = np.zeros((128, 8 * 512), f32)
    return ci


LAUNCHES = [
    [("A", 0), ("F", 0), ("S", 1)],
    [("B", 1), ("F", 1), ("A", 2), ("F", 2), ("S", 3)],
    [("B", 3), ("F", 3)],
]

_PROG_CACHE = {}


def _get_prog(passes):
    key = tuple(passes)
    if key not in _PROG_CACHE:
        _PROG_CACHE[key] = build_program(list(passes), ntok=TOK)
    return _PROG_CACHE[key]


FUSED_PASSES = [("A", 0), ("F", 0), ("S", 1), ("B", 1), ("F", 1), ("A", 2), ("F", 2), ("S", 3), ("B", 3), ("F", 3)]


def kernel(**inputs):
    inp = {k: np.asarray(v) for k, v in inputs.items()}
    key = ("fused",)
    if key not in _PROG_CACHE:
        _PROG_CACHE[key] = build_program(list(FUSED_PASSES), ntok=TOK, fused=True)
    nc = _PROG_CACHE[key]
    sh = shared_inputs(inp, FUSED_PASSES)
    in_maps = []
    for core in range(NCORES):
        ci = core_inputs(inp, core, TOK, FUSED_PASSES)
        ci.pop("st_in", None)
        ci.update(sh)
        in_maps.append(ci)
    res = run_bass_kernel_spmd(nc, in_maps, core_ids=list(range(NCORES)))
    out = np.empty((NB, SEQ, D), np.float32)
    for core in range(NCORES):
        b, half = core // 2, core % 2
        out[b, half * TOK:(half + 1) * TOK, :] = np.asarray(res.results[core]["outT"]).T
    return out


def kernel_unfused(**inputs):
    inp = {k: np.asarray(v) for k, v in inputs.items()}
    cur = None
    st = None
    for li, passes in enumerate(LAUNCHES):
        nc = _get_prog(passes)
        sh = shared_inputs(inp, passes)
        in_maps = []
        for core in range(NCORES):
            ci = core_inputs(inp, core, TOK, passes)
            if cur is not None:
                ci["xT"] = cur[core]
            if st is not None and "st_in" in ci:
                ci["st_in"] = st[core - 1] if core % 2 == 1 else np.zeros_like(st[core])
            ci.update(sh)
            in_maps.append(ci)
        res = run_bass_kernel_spmd(nc, in_maps, core_ids=list(range(NCORES)))
        cur = [np.asarray(r["outT"]) for r in res.results]
        if "st_out" in res.results[0]:
            st = [np.asarray(r["st_out"]) for r in res.results]
    out = np.empty((NB, SEQ, D), np.float32)
    for core in range(NCORES):
        b, half = core // 2, core % 2
        out[b, half * TOK:(half + 1) * TOK, :] = cur[core].T
    return out
```

```python
import numpy as np
import concourse.bass as bass
import concourse.mybir as mybir
from concourse.bass_utils import run_bass_kernel_spmd

F32 = mybir.dt.float32
BF16 = mybir.dt.bfloat16
I32 = mybir.dt.int32
AF = mybir.ActivationFunctionType
ALU = mybir.AluOpType

D = 1024
KT = 8
SEQ = 8192
NB = 4
DEPTH = 4
TOK = 4096
CH = 128
NCORES = 8
FFN = 4096
H = 4
DK = 256
DV = 512
GAMMA = [1.0 - 2.0 ** (-5.0 - h) for h in range(H)]


class Buf:
    __slots__ = ("name", "w", "r")

    def __init__(self, name=""):
        self.name = name
        self.w = None
        self.r = []


class Op:
    __slots__ = ("eng", "fn", "deps", "signal", "sval", "dma", "dsem", "dval", "dprev", "cc")

    def __init__(self, eng, fn, dma):
        self.eng = eng
        self.fn = fn
        self.deps = []
        self.signal = False
        self.sval = 0
        self.dma = dma
        self.dsem = None
        self.dval = 0
        self.dprev = 0
        self.cc = False


ENGS = ["pe", "act", "dve", "pool", "sp"]


class Sched:
    def __init__(self, nc):
        self.nc = nc
        self.ops = {e: [] for e in ENGS}
        self.last = {e: None for e in ENGS}
        self.all_dmas = []

    dry = False
    _dummy = None

    def add(self, eng, fn, r=(), w=(), dma=False, extra=()):
        if self.dry:
            if Sched._dummy is None:
                Sched._dummy = Op("pe", None, False)
            return Sched._dummy
        op = Op(eng, fn, dma)
        deps = []
        for b in r:
            if b.w is not None:
                deps.append(b.w)
        for b in w:
            if b.w is not None:
                deps.append(b.w)
            lastr = {}
            for o in b.r:
                if o.dma:
                    deps.append(o)
                else:
                    lastr[o.eng] = o
            deps.extend(lastr.values())
        deps.extend(extra)
        seen = set()
        for d in deps:
            if d is op or id(d) in seen:
                continue
            seen.add(id(d))
            if (not d.dma) and d.eng == eng:
                if eng == "pe":
                    continue
            op.deps.append(d)
            d.signal = True
        for b in r:
            b.r.append(op)
        for b in w:
            b.w = op
            b.r = []
        self.ops[eng].append(op)
        if fn is not None:
            self.last[eng] = op
        if dma:
            self.all_dmas.append(op)
        return op

    def barrier(self):
        if self.dry:
            return
        lasts = [o for o in self.last.values() if o is not None] + list(self.all_dmas)
        self.all_dmas = []
        for e in ENGS:
            self.add(e, None, extra=lasts)

    def emit(self, stack, final_waits):
        nc = self.nc
        esem = {e: stack.enter_context(nc.semaphore("sem_" + e)) for e in ENGS}
        NDS = 24
        dsems = {e: [stack.enter_context(nc.semaphore("dsem_%s_%d" % (e, i))) for i in range(NDS)]
                 for e in ("sp", "pool", "act")}
        for e in ENGS:
            cnt = 0
            for op in self.ops[e]:
                if not op.dma and op.signal:
                    cnt += 1
                    op.sval = cnt
        ccsem = stack.enter_context(nc.semaphore("sem_cc"))
        ccval = 0
        dcount = {e: 0 for e in dsems}
        dval = {e: [0] * NDS for e in dsems}
        for e in dsems:
            for op in self.ops[e]:
                if op.cc:
                    op.dsem = ccsem
                    op.dprev = ccval
                    ccval += 1
                    op.dval = ccval
                elif op.dma:
                    i = dcount[e] % NDS
                    dcount[e] += 1
                    op.dsem = dsems[e][i]
                    op.dprev = dval[e][i]
                    dval[e][i] += 16
                    op.dval = dval[e][i]
        block = stack.enter_context(nc.Block())
        engobj = {"pe": block.tensor, "act": block.scalar, "dve": block.vector, "pool": block.gpsimd,
                  "sp": block.sync}

        def run(e):
            def body(eng):
                waited = {}
                semobj = {}
                for op in self.ops[e]:
                    needs = {}
                    for d in op.deps:
                        if d.dma:
                            s, v = d.dsem, d.dval
                        else:
                            s, v = esem[d.eng], d.sval
                        k = id(s)
                        semobj[k] = s
                        if needs.get(k, 0) < v:
                            needs[k] = v
                    if op.dma and op.dprev > 0:
                        k = id(op.dsem)
                        semobj[k] = op.dsem
                        if needs.get(k, 0) < op.dprev:
                            needs[k] = op.dprev
                    for k, v in needs.items():
                        if waited.get(k, 0) < v:
                            eng.wait_ge(semobj[k], v)
                            waited[k] = v
                    if op.fn is None:
                        continue
                    ins = op.fn(eng)
                    if op.cc:
                        ins.then_inc(op.dsem, 1)
                    elif op.dma:
                        ins.then_inc(op.dsem, 16)
                    elif op.signal:
                        ins.then_inc(esem[e], 1)
                if e == "sp":
                    for op in final_waits:
                        eng.wait_ge(op.dsem, op.dval)
            return body

        for e in ENGS:
            if self.ops[e] or e == "sp":
                engobj[e](run(e))


from contextlib import ExitStack

TT_OF = {"F": 1024, "A": 512, "B": 512, "S": 512}
RING_SLOTS = 4
UNIT = 4096


def units_of(kind):
    return {"F": 16, "A": 12, "B": 16, "S": 6}[kind]


UNIT_ORDER = {"F": list(range(16)), "A": [4, 5, 6, 7, 0, 1, 2, 3, 8, 9, 10, 11], "B": list(range(16)),
              "S": list(range(6))}


class Arena:
    def __init__(self, nc, nbytes):
        self.t = nc.alloc_sbuf_tensor("arena", [128, nbytes // 4], F32)
        self.off = 0
        self.cap = nbytes

    def mark(self):
        return self.off

    def reset(self, m):
        self.off = m

    def view(self, shape, dt):
        esz = 2 if dt == BF16 else 4
        n = 1
        for s_ in shape[1:]:
            n *= s_
        nbytes = (n * esz + 31) // 32 * 32
        assert self.off + nbytes <= self.cap, ("arena overflow", self.off, nbytes, self.cap)
        ap = self.t[0:shape[0], self.off // 4: self.off // 4 + nbytes // 4]
        self.off += nbytes
        if dt != F32:
            ap = ap.bitcast(dt)
        ap = ap[:, 0:n]
        if len(shape) == 3:
            ap = ap.rearrange("p (a b) -> p a b", a=shape[1])
        elif len(shape) == 4:
            ap = ap.rearrange("p (a b c) -> p a b c", a=shape[1], b=shape[2])
        return ap


def build_program(passes, ntok=TOK, first_from_input=True, dbg=None, fused=False):
    if fused:
        nc = bass.Bass("TRN2", target_bir_lowering=False, num_devices=NCORES)
    else:
        nc = bass.Bass("TRN2", target_bir_lowering=False)
    S = Sched(nc)
    stack = ExitStack()
    kinds = [k for k, _ in passes]

    def din(name, shape, dt=F32):
        return nc.dram_tensor(name, list(shape), dt, kind="ExternalInput").ap()

    def dout(name, shape, dt=F32):
        return nc.dram_tensor(name, list(shape), dt, kind="ExternalOutput").ap()

    xT_d = din("xT", [D, ntok])
    outT_d = dout("outT", [D, ntok])
    cT_d = din("cT", [128, KT])
    adaw_d = din("adaw", [DEPTH, 8, 128, KT * 768])
    adab_d = din("adab", [DEPTH, 128, 48])
    gains_d = din("gains", [DEPTH, 128, 32])
    ident_d = din("ident", [128, 128])
    wd = []
    for pi, (k, l) in enumerate(passes):
        wd.append(din("w%d" % pi, [units_of(k), 128, UNIT]))
    hasA = "A" in kinds
    hasB = ("B" in kinds) or ("S" in kinds)
    if hasA:
        wst_d = din("wst", [2, 128, 8 * 128])
        maskT_d = din("maskT", [128, 128])
        lngT_d = din("lngT", [2, 128, 16])
        lnbT_d = din("lnbT", [2, 128, 16])
        bsj_d = din("bsj", [2, 128, 16 * 128])
    if hasB:
        cos_d = din("cosT", [128, ntok])
        sin_d = din("sinT", [128, ntok])
        dmask_d = din("dmask", [128, H * 128])
        epsp_d = din("epsp", [128, H])
        dk_d = din("dk", [128, H])
        gng_d = din("gng", [2, 128, 16])
        gnb_d = din("gnb", [2, 128, 16])
        flag_d = din("flag", [128, 1])
        if not fused:
            st_in_d = din("st_in", [128, 8 * 512])
            st_out_d = dout("st_out", [128, 8 * 512])
        xch = {"src": None, "gath": None, "n": 0}
        B_xs, B_xg = Buf("xsrc"), Buf("xgath")
        if fused:
            NTB = ntok // 512
            kc_d = nc.dram_tensor("kcache", [NTB, 128, 8 * 512], BF16, kind="Internal").ap()
            vc_d = nc.dram_tensor("vcache", [NTB, 128, 4 * 2048], BF16, kind="Internal").ap()
            B_kc = [Buf("kc%d" % i) for i in range(NTB)]
            B_vc = [Buf("vc%d" % i) for i in range(NTB)]
    dbg_d = {}
    if dbg:
        for name, shape in dbg.items():
            dbg_d[name] = dout(name, shape)

    AR = Arena(nc, 206 * 1024)
    ps = nc.alloc_psum_tensor("ps", [128, 4096], F32)
    PSB = [Buf("ps%d" % i) for i in range(8)]
    psn = [0]

    def nb():
        b = psn[0] % 8
        psn[0] += 1
        return b

    def bank(b):
        return ps[:, b * 512:(b + 1) * 512]

    def mm(out, lhsT, rhs, start, stop, r, w):
        return S.add("pe", lambda e: e.matmul(out, lhsT=lhsT, rhs=rhs, start=start, stop=stop), r=r, w=w)

    def tr(out, in_, ident, r, w):
        return S.add("pe", lambda e: e.transpose(out, in_, ident), r=r, w=w)

    def act(out, in_, func, r, w, scale=None, bias=None, accum=None):
        kw = {}
        if scale is not None:
            kw["scale"] = scale
        if bias is not None:
            kw["bias"] = bias
        if accum is not None:
            kw["accum_out"] = accum
        return S.add("act", lambda e: e.activation(out=out, in_=in_, func=func, **kw), r=r, w=w)

    def tt(eng, out, in0, in1, op, r, w):
        return S.add(eng, lambda e: e.tensor_tensor(out=out, in0=in0, in1=in1, op=op), r=r, w=w)

    def ts(eng, out, in0, s1, s2, op0, op1, r, w):
        if op1 is None:
            return S.add(eng, lambda e: e.tensor_scalar(out=out, in0=in0, scalar1=s1, scalar2=None, op0=op0),
                         r=r, w=w)
        return S.add(eng, lambda e: e.tensor_scalar(out=out, in0=in0, scalar1=s1, scalar2=s2, op0=op0, op1=op1),
                     r=r, w=w)

    def stt(out, in0, scalar, in1, op0, op1, r, w):
        return S.add("dve", lambda e: e.scalar_tensor_tensor(out=out, in0=in0, scalar=scalar, in1=in1,
                                                             op0=op0, op1=op1), r=r, w=w)

    def cp(eng, out, in_, r, w):
        return S.add(eng, lambda e: e.tensor_copy(out=out, in_=in_), r=r, w=w)

    def dma(eng, out, in_, r, w):
        return S.add(eng, lambda e: e.dma_start(out=out, in_=in_), r=r, w=w, dma=True)

    def newton_rsqrt(y, a, t, By, Ba, Bt):
        yi = y.bitcast(I32)
        ai = a.bitcast(I32)
        ts("dve", yi, ai, 1, None, ALU.arith_shift_right, None, r=[Ba], w=[By])
        ts("dve", yi, yi, -1, 0x5F3759DF, ALU.mult, ALU.add, r=[By], w=[By])
        for _ in range(3):
            tt("dve", t, y, y, ALU.mult, r=[By], w=[Bt])
            tt("dve", t, t, a, ALU.mult, r=[Bt, Ba], w=[Bt])
            ts("dve", t, t, -0.5, 1.5, ALU.mult, ALU.add, r=[Bt], w=[Bt])
            tt("dve", y, y, t, ALU.mult, r=[By, Bt], w=[By])

    def sqrt_recip(r_, a_, Br, Ba):
        act(a_, a_, AF.Sqrt, r=[Ba], w=[Ba])
        S.add("dve", lambda e: e.reciprocal(out=r_, in_=a_), r=[Ba], w=[Br])

    onesb = AR.view([128, 128], BF16)
    identb = AR.view([128, 128], BF16)
    identf = AR.view([128, 128], F32)
    cact = AR.view([128, KT], F32)
    modT = AR.view([128, DEPTH, 48], F32)
    gains = AR.view([128, DEPTH, 32], F32)
    prm = AR.view([128, DEPTH, 48], F32)
    ring = [AR.view([128, UNIT], BF16) for _ in range(RING_SLOTS)]
    RB = [Buf("ring%d" % i) for i in range(RING_SLOTS)]
    eps6 = AR.view([128, 1], F32)
    B_const = Buf("const")
    B_prm = Buf("prm")
    persist_mark = AR.mark()

    wseq = []
    wstate = {"issued": 0, "n": 0}
    PF = 3

    def wnext(tag):
        if S.dry:
            wseq.append((wd[tag[0]][tag[2]], tag))
            n = len(wseq) - 1
            return ring[n % RING_SLOTS], RB[n % RING_SLOTS]
        while wstate["issued"] < min(len(wseq), wstate["n"] + PF + 1):
            i = wstate["issued"]
            dma("pool", ring[i % RING_SLOTS], wseq[i][0], r=[], w=[RB[i % RING_SLOTS]])
            wstate["issued"] += 1
        n = wstate["n"]
        assert wseq[n][1] == tag, (wseq[n][1], tag)
        wstate["n"] += 1
        return ring[n % RING_SLOTS], RB[n % RING_SLOTS]

    S.add("pool", lambda e: e.memset(onesb, 1.0), w=[B_const])
    S.add("pool", lambda e: e.memset(eps6, 1e-6), w=[B_const])
    dma("sp", identf, ident_d, r=[], w=[B_const])
    cp("dve", identb, identf, r=[B_const], w=[B_const])
    dma("sp", cact, cT_d, r=[], w=[B_prm])
    act(cact, cact, AF.Silu, r=[B_prm], w=[B_prm])
    dma("sp", gains, gains_d.rearrange("l p k -> p l k"), r=[], w=[B_prm])
    m0 = AR.mark()
    stage = [AR.view([128, KT, 768], BF16) for _ in range(3)]
    cactb = AR.view([128, KT], BF16)
    cp("dve", cactb, cact, r=[B_prm], w=[B_prm])
    SB_ = [Buf("stage0"), Buf("stage1"), Buf("stage2")]
    adabT = AR.view([128, DEPTH, 48], F32)
    dma("sp", adabT, adab_d.rearrange("l p k -> p l k"), r=[], w=[B_prm])
    layers_needed = sorted(set(l for _, l in passes))
    si = 0
    for l in layers_needed:
        pb = nb()
        for blk in range(8):
            st_, sb_ = stage[si % 3], SB_[si % 3]
            si += 1
            dma("pool", st_, adaw_d[l, blk].rearrange("p (k c) -> p k c", k=KT), r=[], w=[sb_])
            for jj in range(6):
                j = blk * 6 + jj
                for kt in range(KT):
                    mm(bank(pb)[:, j:j + 1], st_[:, kt, jj * 128:(jj + 1) * 128], cactb[:, kt:kt + 1],
                       kt == 0, kt == KT - 1, r=[sb_, B_prm], w=[PSB[pb]])
        tt("dve", modT[:, l, :], bank(pb)[:, 0:48], adabT[:, l, :], ALU.add, r=[PSB[pb], B_prm], w=[B_prm])
        for half, (g_pre, g_post) in enumerate(((0, 1), (2, 3))):
            mo = half * 24
            po = half * 24
            stt(prm[:, l, po:po + 8], modT[:, l, mo + 8:mo + 16], 1.0, gains[:, l, g_pre * 8:g_pre * 8 + 8],
                ALU.add, ALU.mult, r=[B_prm], w=[B_prm])
            cp("dve", prm[:, l, po + 8:po + 16], modT[:, l, mo:mo + 8], r=[B_prm], w=[B_prm])
            tt("dve", prm[:, l, po + 16:po + 24], modT[:, l, mo + 16:mo + 24],
               gains[:, l, g_post * 8:g_post * 8 + 8], ALU.mult, r=[B_prm], w=[B_prm])
    if dbg and "prm" in dbg:
        dma("sp", dbg_d["prm"], prm.rearrange("p l k -> p (l k)"), r=[B_prm], w=[])
    S.barrier()
    AR.reset(m0)

    XB = [Buf("x%d" % i) for i in range(ntok // 512)]
    state = {"first": first_from_input}
    final_dmas = []

    def make_common(TT, nh=1):
        c = {}
        c["xin"] = AR.view([128, KT, 512], F32)
        c["sq"] = AR.view([128, KT, 512], BF16)
        c["hTs"] = [AR.view([128, KT, TT], BF16) for _ in range(nh)]
        c["hT"] = c["hTs"][0]
        c["a"] = AR.view([128, 512], F32)
        c["rs"] = AR.view([128, 512], F32)
        c["t"] = AR.view([128, 512], F32)
        c["yT"] = AR.view([128, KT, TT], F32)
        c["xres"] = [AR.view([128, 512], F32) for _ in range(2)]
        for n in ("xin", "sq", "a", "rs", "t"):
            c["B_" + n] = Buf(n)
        c["B_hTs"] = [Buf("hT%d" % i) for i in range(nh)]
        c["B_hT"] = c["B_hTs"][0]
        c["B_yT"] = [[Buf("yT") for _ in range(TT // 512)] for _ in range(KT)]
        c["B_xres"] = [Buf("xres0"), Buf("xres1")]
        c["B_sqk"] = [Buf("sq%d" % i) for i in range(KT)]
        c["xr"] = 0
        c["pend"] = []
        return c

    epsT = AR_eps = None

    def rstd_from_bank(c, b, eps):
        act(c["a"], bank(b), AF.Sqrt, r=[PSB[b], B_const], w=[c["B_a"]], scale=1.0 / D, bias=eps6[:, 0:1])
        S.add("dve", lambda e: e.reciprocal(out=c["rs"], in_=c["a"]), r=[c["B_a"]], w=[c["B_rs"]])

    def rms_stats(c, src3, Bsrc_list, eps):
        b = nb()
        for kt in range(KT):
            act(c["sq"][:, kt, :], src3[:, kt, :], AF.Square, r=[Bsrc_list[kt]], w=[c["B_sqk"][kt]])
            mm(bank(b), onesb, c["sq"][:, kt, :], kt == 0, kt == KT - 1, r=[c["B_sqk"][kt], B_const], w=[PSB[b]])
        rstd_from_bank(c, b, eps)

    def pre_norm(c, src_d, tok0, hoff, gs, sh):
        xin = c["xin"]
        dma("sp", xin, src_d.rearrange("(k p) t -> p k t", p=128)[:, :, tok0:tok0 + 512],
            r=[XB[tok0 // 512]], w=[c["B_xin"]])
        rms_stats(c, xin, [c["B_xin"]] * KT, 1e-6)
        for kt in range(KT):
            tt("dve", xin[:, kt, :], xin[:, kt, :], c["rs"], ALU.mult, r=[c["B_xin"], c["B_rs"]], w=[c["B_xin"]])
        for kt in range(KT):
            act(c["hT"][:, kt, hoff:hoff + 512], xin[:, kt, :], AF.Identity, r=[c["B_xin"], B_prm], w=[c["B_hT"]],
                scale=gs[:, kt:kt + 1], bias=sh[:, kt:kt + 1])

    def evac_y(c, b, dt, hf, gg, sbank):
        sl = slice(hf * 512, (hf + 1) * 512)
        act(c["sq"][:, dt, :], bank(b), AF.Square, r=[PSB[b]], w=[c["B_sqk"][dt]])
        act(c["yT"][:, dt, sl], bank(b), AF.Copy, r=[PSB[b], B_prm], w=[c["B_yT"][dt][hf]], scale=gg[:, dt:dt + 1])
        c["pend"].append((dt, sbank))

    def flush_pend(c, keep=0):
        while len(c["pend"]) > keep:
            dt, sbank = c["pend"].pop(0)
            mm(bank(sbank), onesb, c["sq"][:, dt, :], dt == 0, dt == KT - 1, r=[c["B_sqk"][dt], B_const],
               w=[PSB[sbank]])

    def post_norm_store(c, src_d, dst_d, tok0, hf, sbank):
        flush_pend(c)
        yh = c["yT"][:, :, hf * 512:(hf + 1) * 512]
        rstd_from_bank(c, sbank, 1e-6)
        for kt in range(KT):
            k = c["xr"] % 2
            c["xr"] += 1
            dma("sp", c["xres"][k], src_d[kt * 128:(kt + 1) * 128, tok0:tok0 + 512],
                r=[XB[tok0 // 512]], w=[c["B_xres"][k]])
            tt("dve", yh[:, kt, :], yh[:, kt, :], c["rs"], ALU.mult,
               r=[c["B_yT"][kt][hf], c["B_rs"]], w=[c["B_yT"][kt][hf]])
            tt("dve", yh[:, kt, :], yh[:, kt, :], c["xres"][k], ALU.add,
               r=[c["B_yT"][kt][hf], c["B_xres"][k]], w=[c["B_yT"][kt][hf]])
        o = dma("sp", dst_d.rearrange("(k p) t -> p k t", p=128)[:, :, tok0:tok0 + 512], yh,
                r=[c["B_yT"][kt][hf] for kt in range(KT)], w=[XB[tok0 // 512]])
        final_dmas.append(o)

    def pass_F(pi, l):
        TT = TT_OF["F"]
        NH = TT // 512
        src = xT_d if state["first"] else outT_d
        m = AR.mark()
        c = make_common(TT)
        aT = AR.view([128, 32, TT], BF16)
        B_aT = [Buf("aT%d" % i) for i in range(32)]
        rt = [AR.view([128, 512], F32) for _ in range(2)]
        B_rt = [Buf("rt0"), Buf("rt1")]
        gs, sh, gg = prm[:, l, 24:32], prm[:, l, 32:40], prm[:, l, 40:48]
        nt = ntok // TT
        rk = [0]

        def pre(ti):
            for hf in range(NH):
                pre_norm(c, src, ti * TT + hf * 512, hf * 512, gs, sh)

        def inproj(ti):
            for u in range(8):
                wap, wb = wnext((pi, ti, u))
                w3 = wap.rearrange("p (k c) -> p k c", k=KT)
                for f in range(4):
                    ft = u * 4 + f
                    for hf in range(NH):
                        b = nb()
                        for kt in range(KT):
                            mm(bank(b), w3[:, kt, f * 128:(f + 1) * 128], c["hT"][:, kt, hf * 512:(hf + 1) * 512],
                               kt == 0, kt == KT - 1, r=[wb, c["B_hT"]], w=[PSB[b]])
                        k = rk[0] % 2
                        rk[0] += 1
                        act(rt[k], bank(b), AF.Relu, r=[PSB[b]], w=[B_rt[k]])
                        act(aT[:, ft, hf * 512:(hf + 1) * 512], rt[k], AF.Square, r=[B_rt[k]], w=[B_aT[ft]])

        def outproj(ti):
            sb = [nb() for _ in range(NH)]
            for dt in range(KT):
                wap, wb = wnext((pi, ti, 8 + dt))
                w3 = wap.rearrange("p (j c) -> p j c", j=32)
                for hf in range(NH):
                    b = nb()
                    while b in sb:
                        b = nb()
                    for j in range(32):
                        mm(bank(b), w3[:, j, :], aT[:, j, hf * 512:(hf + 1) * 512], j == 0, j == 31,
                           r=[wb, B_aT[j]], w=[PSB[b]])
                    flush_pend(c)
                    evac_y(c, b, dt, hf, gg, sb[hf])
            for hf in range(NH):
                post_norm_store(c, src, outT_d, ti * TT + hf * 512, hf, sb[hf])

        pre(0)
        for ti in range(nt):
            inproj(ti)
            if ti + 1 < nt:
                pre(ti + 1)
            outproj(ti)
        S.barrier()
        AR.reset(m)
        state["first"] = False


    def pass_A(pi, l):
        TT = TT_OF["A"]
        assert TT == 512
        jA = l // 2
        src = xT_d if state["first"] else outT_d
        m = AR.mark()
        c = make_common(TT, nh=2)

        def sel_h(ti):
            c["hT"] = c["hTs"][ti % 2]
            c["B_hT"] = c["B_hTs"][ti % 2]

        uT = AR.view([128, 16, 512], BF16)
        B_uT = [Buf("uT%d" % i) for i in range(16)]
        vgbs = [AR.view([128, 4, 2048], BF16) for _ in range(2)]
        B_vgbs = [[Buf("vgb%d_%d" % (p_, i)) for i in range(4)] for p_ in range(2)]
        gt = [AR.view([128, 512], F32) for _ in range(3)]
        B_gt = [Buf("gt%d" % i) for i in range(3)]
        vn = [AR.view([128, 2048], BF16) for _ in range(2)]
        B_vn = [Buf("vn0"), Buf("vn1")]
        sT = AR.view([128, 16, 512], BF16)
        B_sT = [Buf("sT%d" % i) for i in range(4)]
        lngT = AR.view([128, 16], F32)
        lnbT = AR.view([128, 16], F32)
        bsj = AR.view([128, 16, 128], F32)
        WsT = AR.view([128, 8, 128], BF16)
        wtmp = AR.view([128, 8, 128], F32)
        maskT = AR.view([128, 128], F32)
        sts = [AR.view([128, 4, 24], F32) for _ in range(2)]
        mvs = [AR.view([128, 4, 2], F32) for _ in range(2)]
        a4s = [AR.view([128, 4], F32) for _ in range(2)]
        r4s = [AR.view([128, 4], F32) for _ in range(2)]
        n4s = [AR.view([128, 4], F32) for _ in range(2)]
        B_cA = Buf("constA")
        B_sts = [Buf("st0"), Buf("st1")]
        B_mvs = [Buf("mv0"), Buf("mv1")]
        B_a4s = [Buf("a40"), Buf("a41")]
        B_r4s = [Buf("r40"), Buf("r41")]
        B_n4s = [Buf("n40"), Buf("n41")]
        gs, sh, gg = prm[:, l, 0:8], prm[:, l, 8:16], prm[:, l, 16:24]
        dma("sp", lngT, lngT_d[jA], r=[], w=[B_cA])
        dma("sp", lnbT, lnbT_d[jA], r=[], w=[B_cA])
        dma("sp", bsj, bsj_d[jA].rearrange("p (j t) -> p j t", j=16), r=[], w=[B_cA])
        dma("sp", wtmp, wst_d[jA].rearrange("p (g t) -> p g t", g=8), r=[], w=[B_cA])
        dma("sp", maskT, maskT_d, r=[], w=[B_cA])
        for g in range(8):
            tt("dve", WsT[:, g, :], wtmp[:, g, :], maskT, ALU.mult, r=[B_cA], w=[B_cA])
        for gb in range(2):
            b = nb()
            for g4 in range(4):
                g = gb * 4 + g4
                mm(bank(b)[:, g4 * 128:(g4 + 1) * 128], onesb, WsT[:, g, :], True, True, r=[B_cA, B_const],
                   w=[PSB[b]])
            for g4 in range(4):
                g = gb * 4 + g4
                for j in (2 * g, 2 * g + 1):
                    stt(bsj[:, j, :], bank(b)[:, g4 * 128:(g4 + 1) * 128], lnbT[:, j:j + 1], bsj[:, j, :],
                        ALU.mult, ALU.add, r=[PSB[b], B_cA], w=[B_cA])
        nt = ntok // TT
        ks = {"gt": 0, "t1": 0, "se": 0}

        def uproj_unit(ti, u):
            wap, wb = wnext((pi, ti, u))
            w3 = wap.rearrange("p (k c) -> p k c", k=KT)
            for f in range(4):
                ft = u * 4 + f
                b = nb()
                for kt in range(KT):
                    mm(bank(b), w3[:, kt, f * 128:(f + 1) * 128], c["hT"][:, kt, :], kt == 0, kt == KT - 1,
                       r=[wb, c["B_hT"]], w=[PSB[b]])
                act(uT[:, ft, :], bank(b), AF.Gelu_apprx_tanh, r=[PSB[b]], w=[B_uT[ft]])

        def vproj(ti):
            pp = ti % 2
            vgb, B_vgb = vgbs[pp], B_vgbs[pp]
            st, mv, a4, r4, n4 = sts[pp], mvs[pp], a4s[pp], r4s[pp], n4s[pp]
            B_st, B_mv, B_a4, B_r4, B_n4 = B_sts[pp], B_mvs[pp], B_a4s[pp], B_r4s[pp], B_n4s[pp]
            for u in range(4):
                wap, wb = wnext((pi, ti, 4 + u))
                w3 = wap.rearrange("p (k c) -> p k c", k=KT)
                for ch in range(4):
                    b = nb()
                    for kt in range(KT):
                        mm(bank(b), c["hT"][:, kt, ch * 128:(ch + 1) * 128], w3[:, kt, :], kt == 0, kt == KT - 1,
                           r=[wb, c["B_hT"]], w=[PSB[b]])
                    k = ks["gt"] % 3
                    ks["gt"] += 1
                    act(gt[k], bank(b), AF.Gelu_apprx_tanh, r=[PSB[b]], w=[B_gt[k]])
                    S.add("dve", lambda e, o=st[:, ch, u * 6:(u + 1) * 6], i=gt[k]: e.bn_stats(out=o, in_=i),
                          r=[B_gt[k]], w=[B_st])
                    act(vgb[:, ch, u * 512:(u + 1) * 512], gt[k], AF.Copy, r=[B_gt[k]], w=[B_vgb[ch]])
            for ch in range(4):
                S.add("dve", lambda e, o=mv[:, ch, :], i=st[:, ch, :]: e.bn_aggr(out=o, in_=i), r=[B_st], w=[B_mv])
            ts("dve", a4, mv[:, :, 1], 1e-5, None, ALU.add, None, r=[B_mv], w=[B_a4])
            sqrt_recip(r4, a4, B_r4, B_a4)
            stt(n4, mv[:, :, 0], -1.0, r4, ALU.mult, ALU.mult, r=[B_mv, B_r4], w=[B_n4])

        def spatial_chunk(ti, ch):
            pp = ti % 2
            vgb, B_vgb = vgbs[pp], B_vgbs[pp]
            r4, n4, B_r4, B_n4 = r4s[pp], n4s[pp], B_r4s[pp], B_n4s[pp]
            k = ks["t1"] % 2
            ks["t1"] += 1
            for hh in range(2):
                act(vn[k][:, hh * 1024:(hh + 1) * 1024], vgb[:, ch, hh * 1024:(hh + 1) * 1024], AF.Identity,
                    r=[B_vgb[ch], B_r4, B_n4], w=[B_vn[k]], scale=r4[:, ch:ch + 1], bias=n4[:, ch:ch + 1])
            for jb in range(4):
                b = nb()
                for jj in range(4):
                    j = jb * 4 + jj
                    mm(bank(b)[:, jj * 128:(jj + 1) * 128], vn[k][:, j * 128:(j + 1) * 128], WsT[:, j // 2, :],
                       True, True, r=[B_vn[k], B_cA], w=[PSB[b]])
                for jj in range(4):
                    j = jb * 4 + jj
                    stt(sT[:, j, ch * 128:(ch + 1) * 128], bank(b)[:, jj * 128:(jj + 1) * 128], lngT[:, j:j + 1],
                        bsj[:, j, :], ALU.mult, ALU.add, r=[PSB[b], B_cA], w=[B_sT[ch]])

        def gate(ti):
            for jb in range(4):
                uv = uT[:, jb * 4:(jb + 1) * 4, :]
                tt("dve", uv, uv, sT[:, jb * 4:(jb + 1) * 4, :], ALU.mult, r=B_sT + B_uT[jb * 4:(jb + 1) * 4],
                   w=B_uT[jb * 4:(jb + 1) * 4])

        def outproj(ti):
            sbank = nb()
            for u in range(4):
                wap, wb = wnext((pi, ti, 8 + u))
                w4 = wap.rearrange("p (g j c) -> p g j c", g=2, j=16)
                for g2 in range(2):
                    dt = u * 2 + g2
                    b = nb()
                    if b == sbank:
                        b = nb()
                    for j in range(16):
                        mm(bank(b), w4[:, g2, j, :], uT[:, j, :], j == 0, j == 15, r=[wb, B_uT[j]], w=[PSB[b]])
                    flush_pend(c)
                    evac_y(c, b, dt, 0, gg, sbank)
            post_norm_store(c, src, outT_d, ti * TT, 0, sbank)

        sel_h(0)
        pre_norm(c, src, 0, 0, gs, sh)
        for ti in range(nt):
            more = ti + 1 < nt
            sel_h(ti)
            vproj(ti)
            uproj_unit(ti, 0)
            uproj_unit(ti, 1)
            spatial_chunk(ti, 0)
            uproj_unit(ti, 2)
            spatial_chunk(ti, 1)
            uproj_unit(ti, 3)
            spatial_chunk(ti, 2)
            spatial_chunk(ti, 3)
            if more:
                sel_h(ti + 1)
                pre_norm(c, src, (ti + 1) * TT, 0, gs, sh)
                sel_h(ti)
            gate(ti)
            outproj(ti)
        S.barrier()
        AR.reset(m)
        state["first"] = False

    def pass_B(pi, l, sweep=False):
        TT = 512
        jB = l // 2
        src = xT_d if state["first"] else outT_d
        m = AR.mark()
        c = make_common(TT)
        csb = [[AR.view([128, 512], F32) for _ in range(2)] for _ in range(2)]
        B_csb = [Buf("cs0"), Buf("cs1")]
        ra = [AR.view([128, 512], F32) for _ in range(2)]
        rb = [AR.view([128, 512], F32) for _ in range(2)]
        B_ra = [Buf("ra0"), Buf("ra1")]
        B_rb = [Buf("rb0"), Buf("rb1")]
        kT = AR.view([128, 8, 512], BF16)
        B_kT = [Buf("kT%d" % i) for i in range(8)]
        v = AR.view([128, 4, 2048], BF16)
        B_v = [Buf("v%d" % i) for i in range(4)]
        kvs = [{"kT": kT, "B_kT": B_kT, "v": v, "B_v": B_v}]
        if sweep:
            kvs.append({"kT": AR.view([128, 8, 512], BF16), "B_kT": [Buf("kTb%d" % i) for i in range(8)],
                        "v": AR.view([128, 4, 2048], BF16), "B_v": [Buf("vb%d" % i) for i in range(4)]})
        cur = dict(kvs[0])

        def select(ti):
            cur.update(kvs[ti % len(kvs)])
        kd = [AR.view([128, 1024], BF16) for _ in range(2)]
        B_kd = [Buf("kd0"), Buf("kd1")]
        stf = AR.view([128, 8, 512], F32)
        B_stf = [Buf("stf%d" % i) for i in range(8)]
        dk = AR.view([128, H], F32)
        B_cB = Buf("constB")
        dma("sp", dk, dk_d, r=[], w=[B_cB])
        if not sweep:
            qT = AR.view([128, 8, 512], BF16)
            B_qT = [Buf("qT%d" % i) for i in range(8)]
            sgT = AR.view([128, 16, 512], BF16)
            B_sg = [Buf("sg%d" % i) for i in range(16)]
            scT = [AR.view([128, 512], BF16) for _ in range(2)]
            B_sc = [Buf("sc0"), Buf("sc1")]
            on = [AR.view([128, 2048], BF16) for _ in range(2)]
            B_on = [Buf("on0"), Buf("on1")]
            ot = [AR.view([128, 8, 128], BF16) for _ in range(2)]
            B_ot = [Buf("ot0"), Buf("ot1")]
            stb = AR.view([128, 8, 512], BF16)
            B_stb = [Buf("stb%d" % i) for i in range(8)]
            dmask = AR.view([128, 512], F32)
            epsp = AR.view([128, H], F32)
            gng = AR.view([128, 16], F32)
            gnb = AR.view([128, 16], F32)
            flag = AR.view([128, 1], F32)
            gst = AR.view([128, 4, H, 6], F32)
            gmv = AR.view([128, 4, H, 2], F32)
            a4 = AR.view([128, 4, H], F32)
            r4 = AR.view([128, 4, H], F32)
            n4 = AR.view([128, 4, H], F32)
            B_gst = [Buf("gst%d" % i) for i in range(4)]
            B_gmv = [Buf("gmv%d" % i) for i in range(4)]
            B_a4 = [Buf("a4%d" % i) for i in range(4)]
            B_r4 = [Buf("r4%d" % i) for i in range(4)]
            B_n4 = [Buf("n4%d" % i) for i in range(4)]
            dma("sp", dmask, dmask_d, r=[], w=[B_cB])
            dma("sp", epsp, epsp_d, r=[], w=[B_cB])
            dma("sp", gng, gng_d[jB], r=[], w=[B_cB])
            dma("sp", gnb, gnb_d[jB], r=[], w=[B_cB])
            dma("sp", flag, flag_d, r=[], w=[B_cB])
            if fused and S.dry:
                pass
            elif fused:
                dma("sp", stf.rearrange("p a b -> p (a b)"), xch["gath"][0:128, :], r=[B_xg], w=B_stf)
            else:
                dma("sp", stf.rearrange("p a b -> p (a b)"), st_in_d, r=[], w=B_stf)
            for i in range(8):
                ts("dve", stf[:, i, :], stf[:, i, :], flag[:, 0:1], None, ALU.mult, None, r=[B_stf[i], B_cB],
                   w=[B_stf[i]])
                cp("pool", stb[:, i, :], stf[:, i, :], r=[B_stf[i]], w=[B_stb[i]])
        else:
            S.add("pool", lambda e: e.memset(stf.rearrange("p a b -> p (a b)"), 0.0), w=B_stf)
        gs, sh, gg = prm[:, l, 0:8], prm[:, l, 8:16], prm[:, l, 16:24]
        nt = ntok // TT
        ks = {"r": 0, "kd": 0, "sc": 0, "on": 0, "ot": 0}
        GC = [GAMMA[h] ** CH for h in range(H)]

        def rot_unit(ti, u_idx, dst, B_dst, hbase):
            wap, wb = wnext((pi, ti, u_idx))
            w3 = wap.rearrange("p (k c) -> p k c", k=KT)
            for hh in range(2):
                h = hbase + hh
                b1, b2 = nb(), nb()
                for bb, f in ((b1, hh * 2), (b2, hh * 2 + 1)):
                    for kt in range(KT):
                        mm(bank(bb), w3[:, kt, f * 128:(f + 1) * 128], c["hT"][:, kt, :], kt == 0, kt == KT - 1,
                           r=[wb, c["B_hT"]], w=[PSB[bb]])
                k = ks["r"] % 2
                ks["r"] += 1
                cs, B_cs = csb[ti % 2], B_csb[ti % 2]
                tt("dve", ra[k], bank(b1), cs[0], ALU.mult, r=[PSB[b1], B_cs], w=[B_ra[k]])
                tt("dve", rb[k], bank(b2), cs[1], ALU.mult, r=[PSB[b2], B_cs], w=[B_rb[k]])
                tt("pool", dst[:, 2 * h, :], ra[k], rb[k], ALU.subtract, r=[B_ra[k], B_rb[k]], w=[B_dst[2 * h]])
                k = ks["r"] % 2
                ks["r"] += 1
                tt("dve", ra[k], bank(b1), cs[1], ALU.mult, r=[PSB[b1], B_cs], w=[B_ra[k]])
                tt("dve", rb[k], bank(b2), cs[0], ALU.mult, r=[PSB[b2], B_cs], w=[B_rb[k]])
                tt("pool", dst[:, 2 * h + 1, :], ra[k], rb[k], ALU.add, r=[B_ra[k], B_rb[k]], w=[B_dst[2 * h + 1]])

        def v_units(ti, u0):
            for u in range(4):
                wap, wb = wnext((pi, ti, u0 + u))
                w3 = wap.rearrange("p (k c) -> p k c", k=KT)
                for ch in range(4):
                    b = nb()
                    for kt in range(KT):
                        mm(bank(b), c["hT"][:, kt, ch * 128:(ch + 1) * 128], w3[:, kt, :], kt == 0, kt == KT - 1,
                           r=[wb, c["B_hT"]], w=[PSB[b]])
                    act(cur["v"][:, ch, u * 512:(u + 1) * 512], bank(b), AF.Copy, r=[PSB[b]],
                        w=[cur["B_v"][ch]])

        def k_transpose(ch):
            k = ks["kd"] % 2
            ks["kd"] += 1
            b = nb()
            bb = bank(b).bitcast(BF16)
            for ft in range(8):
                tr(bb[:, ft * 128:(ft + 1) * 128], cur["kT"][:, ft, ch * 128:(ch + 1) * 128], identb,
                   r=[cur["B_kT"][ft], B_const], w=[PSB[b]])
            for h in range(H):
                act(kd[k][:, h * 256:(h + 1) * 256], bb[:, h * 256:(h + 1) * 256], AF.Copy, r=[PSB[b], B_cB],
                    w=[B_kd[k]], scale=dk[:, h:h + 1])
            return k

        def state_update(ch, k):
            for h in range(H):
                for d2 in range(2):
                    i = 2 * h + d2
                    b = nb()
                    mm(bank(b), kd[k][:, h * 256 + d2 * 128:h * 256 + (d2 + 1) * 128],
                       cur["v"][:, ch, h * 512:(h + 1) * 512], True, True, r=[B_kd[k], cur["B_v"][ch]], w=[PSB[b]])
                    stt(stf[:, i, :], stf[:, i, :], float(GC[h]), bank(b), ALU.mult, ALU.add,
                        r=[B_stf[i], PSB[b]], w=[B_stf[i]])
                    if not sweep:
                        act(stb[:, i, :], stf[:, i, :], AF.Copy, r=[B_stf[i]], w=[B_stb[i]])

        def load_cs(ti):
            cs, B_cs = csb[ti % 2], B_csb[ti % 2]
            dma("sp", cs[0], cos_d[:, ti * TT:(ti + 1) * TT], r=[], w=[B_cs])
            dma("sp", cs[1], sin_d[:, ti * TT:(ti + 1) * TT], r=[], w=[B_cs])

        def proj(ti):
            if sweep:
                rot_unit(ti, 0, cur["kT"], cur["B_kT"], 0)
                rot_unit(ti, 1, cur["kT"], cur["B_kT"], 2)
                if fused:
                    dma("pool", kc_d[ti], cur["kT"].rearrange("p a b -> p (a b)"), r=cur["B_kT"], w=[B_kc[ti]])
                v_units(ti, 2)
                if fused:
                    dma("pool", vc_d[ti], cur["v"].rearrange("p a b -> p (a b)"), r=cur["B_v"], w=[B_vc[ti]])
            else:
                if fused:
                    dma("sp", kT.rearrange("p a b -> p (a b)"), kc_d[ti], r=[B_kc[ti]], w=B_kT)
                    dma("sp", v.rearrange("p a b -> p (a b)"), vc_d[ti], r=[B_vc[ti]], w=B_v)
                rot_unit(ti, 0, qT, B_qT, 0)
                rot_unit(ti, 1, qT, B_qT, 2)
                if not fused:
                    rot_unit(ti, 2, kT, B_kT, 0)
                    rot_unit(ti, 3, kT, B_kT, 2)
                    v_units(ti, 4)
                for u in range(4):
                    wap, wb = wnext((pi, ti, 8 + u))
                    w3 = wap.rearrange("p (k c) -> p k c", k=KT)
                    for f in range(4):
                        ft = u * 4 + f
                        b = nb()
                        for kt in range(KT):
                            mm(bank(b), w3[:, kt, f * 128:(f + 1) * 128], c["hT"][:, kt, :], kt == 0, kt == KT - 1,
                               r=[wb, c["B_hT"]], w=[PSB[b]])
                        act(sgT[:, ft, :], bank(b), AF.Silu, r=[PSB[b]], w=[B_sg[ft]])

        def chunks(ti):
            if sweep:
                k0 = k_transpose(0)
                k1 = k_transpose(1)
                state_update(0, k0)
                k2 = k_transpose(2)
                state_update(1, k1)
                k3 = k_transpose(3)
                state_update(2, k2)
                state_update(3, k3)
                return
            kk = {}
            sck = {}
            onk = {}

            def stage_a(ch):
                tok = slice(ch * 128, (ch + 1) * 128)
                kk[ch] = k_transpose(ch)
                b = nb()
                for h in range(H):
                    for d2 in range(2):
                        i = 2 * h + d2
                        mm(bank(b)[:, h * 128:(h + 1) * 128], kT[:, i, tok], qT[:, i, tok], d2 == 0, d2 == 1,
                           r=[B_kT[i], B_qT[i]], w=[PSB[b]])
                k2 = ks["sc"] % 2
                ks["sc"] += 1
                sck[ch] = k2
                tt("dve", scT[k2], bank(b), dmask, ALU.mult, r=[PSB[b], B_cB], w=[B_sc[k2]])

            def stage_b(ch):
                tok = slice(ch * 128, (ch + 1) * 128)
                k2 = sck[ch]
                k3 = ks["on"] % 2
                ks["on"] += 1
                onk[ch] = k3
                for h in range(H):
                    b = nb()
                    mm(bank(b), scT[k2][:, h * 128:(h + 1) * 128], v[:, ch, h * 512:(h + 1) * 512], True, False,
                       r=[B_sc[k2], B_v[ch]], w=[PSB[b]])
                    for d2 in range(2):
                        i = 2 * h + d2
                        mm(bank(b), qT[:, i, tok], stb[:, i, :], False, d2 == 1, r=[B_qT[i], B_stb[i]],
                           w=[PSB[b]])
                    S.add("dve", lambda e, o=gst[:, ch, h, :], i_=bank(b): e.bn_stats(out=o, in_=i_),
                          r=[PSB[b]], w=[B_gst[ch]])
                    act(on[k3][:, h * 512:(h + 1) * 512], bank(b), AF.Copy, r=[PSB[b]], w=[B_on[k3]])
                state_update(ch, kk[ch])

            def stage_c(ch):
                tok = slice(ch * 128, (ch + 1) * 128)
                k3 = onk[ch]
                for h in range(H):
                    S.add("dve", lambda e, o=gmv[:, ch, h, :], i_=gst[:, ch, h, :]: e.bn_aggr(out=o, in_=i_),
                          r=[B_gst[ch]], w=[B_gmv[ch]])
                tt("dve", a4[:, ch, :], gmv[:, ch, :, 1], epsp, ALU.add, r=[B_gmv[ch], B_cB], w=[B_a4[ch]])
                sqrt_recip(r4[:, ch, :], a4[:, ch, :], B_r4[ch], B_a4[ch])
                stt(n4[:, ch, :], gmv[:, ch, :, 0], -1.0, r4[:, ch, :], ALU.mult, ALU.mult,
                    r=[B_gmv[ch], B_r4[ch]], w=[B_n4[ch]])
                for h in range(H):
                    act(on[k3][:, h * 512:(h + 1) * 512], on[k3][:, h * 512:(h + 1) * 512], AF.Identity,
                        r=[B_on[k3], B_r4[ch], B_n4[ch]], w=[B_on[k3]], scale=r4[:, ch, h:h + 1],
                        bias=n4[:, ch, h:h + 1])
                for eb in range(2):
                    b = nb()
                    bb = bank(b).bitcast(BF16)
                    for e8 in range(8):
                        et = eb * 8 + e8
                        tr(bb[:, e8 * 128:(e8 + 1) * 128], on[k3][:, et * 128:(et + 1) * 128], identb,
                           r=[B_on[k3], B_const], w=[PSB[b]])
                    k4 = ks["ot"] % 2
                    ks["ot"] += 1
                    for e8 in range(8):
                        et = eb * 8 + e8
                        act(ot[k4][:, e8, :], bb[:, e8 * 128:(e8 + 1) * 128], AF.Identity, r=[PSB[b], B_cB],
                            w=[B_ot[k4]], scale=gng[:, et:et + 1], bias=gnb[:, et:et + 1])
                    sv = sgT[:, eb * 8:(eb + 1) * 8, tok]
                    tt("dve", sv, ot[k4], sv, ALU.mult, r=[B_ot[k4]] + B_sg[eb * 8:(eb + 1) * 8],
                       w=B_sg[eb * 8:(eb + 1) * 8])

            stage_a(0)
            stage_a(1)
            stage_b(0)
            stage_a(2)
            stage_b(1)
            stage_c(0)
            stage_a(3)
            stage_b(2)
            stage_c(1)
            stage_b(3)
            stage_c(2)
            stage_c(3)

        def outproj(ti):
            sbank = nb()
            for u in range(4):
                wap, wb = wnext((pi, ti, 12 + u))
                w4 = wap.rearrange("p (g j c) -> p g j c", g=2, j=16)
                for g2 in range(2):
                    dt = u * 2 + g2
                    b = nb()
                    if b == sbank:
                        b = nb()
                    for j in range(16):
                        mm(bank(b), w4[:, g2, j, :], sgT[:, j, :], j == 0, j == 15, r=[wb, B_sg[j]], w=[PSB[b]])
                    flush_pend(c)
                    evac_y(c, b, dt, 0, gg, sbank)
            post_norm_store(c, src, outT_d, ti * TT, 0, sbank)

        load_cs(0)
        pre_norm(c, src, 0, 0, gs, sh)
        if sweep:
            select(0)
            proj(0)
            for ti in range(nt):
                if ti + 1 < nt:
                    load_cs(ti + 1)
                    pre_norm(c, src, (ti + 1) * TT, 0, gs, sh)
                    select(ti + 1)
                    proj(ti + 1)
                select(ti)
                chunks(ti)
        else:
            for ti in range(nt):
                proj(ti)
                if ti + 1 < nt:
                    load_cs(ti + 1)
                    pre_norm(c, src, (ti + 1) * TT, 0, gs, sh)
                chunks(ti)
                outproj(ti)
        if sweep and fused and S.dry:
            pass
        elif sweep and fused:
            xch["n"] += 1
            xch["src"] = nc.dram_tensor("xsrc%d" % xch["n"], [128, 4096], F32, kind="Internal").ap()
            xch["gath"] = nc.dram_tensor("xgath%d" % xch["n"], [256, 4096], F32, kind="Internal").ap()
            dma("sp", xch["src"], stf.rearrange("p a b -> p (a b)"), r=B_stf, w=[B_xs])
            xs_, xg_ = xch["src"], xch["gath"]
            o = S.add("pool", lambda e: e.collective_compute(
                "AllGather", ALU.bypass, replica_groups=[[0, 1], [2, 3], [4, 5], [6, 7]],
                ins=[xs_[:, :]], outs=[xg_[:, :]]), r=[B_xs], w=[B_xg], dma=True)
            o.cc = True
        elif sweep:
            o = dma("sp", st_out_d, stf.rearrange("p a b -> p (a b)"), r=B_stf, w=[])
        S.barrier()
        AR.reset(m)
        if not sweep:
            state["first"] = False

    def pass_S(pi, l):
        pass_B(pi, l, sweep=True)

    PASS_FN = {"F": pass_F, "A": pass_A, "B": pass_B, "S": pass_S}
    S.dry = True
    sv_first, sv_psn = state["first"], psn[0]
    for pi, (k, l) in enumerate(passes):
        PASS_FN[k](pi, l)
    S.dry = False
    state["first"], psn[0] = sv_first, sv_psn
    del final_dmas[:]
    for pi, (k, l) in enumerate(passes):
        PASS_FN[k](pi, l)
    assert wstate["n"] == len(wseq), (wstate["n"], len(wseq))

    S.emit(stack, final_dmas[-(ntok // 512):] if final_dmas else [])
    stack.close()
    return nc


def tile_in(w):
    K, F_ = w.shape
    assert K == 1024
    return np.ascontiguousarray(w.reshape(8, 128, F_ // 512, 512).transpose(2, 1, 0, 3)).reshape(F_ // 512, 128, 4096)


def tile_out(w):
    K, N = w.shape
    assert N == 1024
    J = K // 128
    g = 32 // J
    t = w.reshape(J, 128, 8 // g, g, 128).transpose(2, 1, 3, 0, 4)
    return np.ascontiguousarray(t).reshape(8 // g, 128, 4096)


def pass_weights(kind, l, inp):
    j = l // 2
    if kind == "F":
        return np.concatenate([tile_in(inp["ffn_w_in"][l]), tile_out(inp["ffn_w_out"][l])], 0)
    if kind == "A":
        return np.concatenate([tile_in(inp["sg_w_in"][j]), tile_out(inp["sg_w_out"][j])], 0)
    if kind == "B":
        return np.concatenate([tile_in(inp["ret_w_in"][j]), tile_out(inp["ret_w_out"][j])], 0)
    if kind == "S":
        w = inp["ret_w_in"][j]
        return tile_in(np.ascontiguousarray(w[:, 1024:4096]))
    raise ValueError(kind)


def rope_tables():
    half = DK // 2
    inv_freq = (np.float32(10000.0) ** (-np.arange(half, dtype=np.float32) / np.float32(half))).astype(np.float32)
    pos = np.arange(SEQ, dtype=np.float32)
    ang = (pos[:, None] * inv_freq[None, :]).astype(np.float32)
    return np.cos(ang).astype(np.float32).T.copy(), np.sin(ang).astype(np.float32).T.copy()


def shared_inputs(inp, passes):
    f32 = np.float32
    sh = {}
    aw = np.asarray(inp["ada_w"], f32)
    sh["adaw"] = np.ascontiguousarray(aw.reshape(DEPTH, KT, 128, 8, 768).transpose(0, 3, 2, 1, 4)).reshape(
        DEPTH, 8, 128, KT * 768)
    sh["adab"] = np.ascontiguousarray(np.asarray(inp["ada_b"], f32).reshape(DEPTH, 48, 128).transpose(0, 2, 1))
    g = np.stack([np.asarray(inp[k], f32) for k in ("pre_mix_g", "post_mix_g", "pre_ffn_g", "post_ffn_g")], 1)
    sh["gains"] = np.ascontiguousarray(g.reshape(DEPTH, 4, KT, 128).transpose(0, 3, 1, 2)).reshape(DEPTH, 128, 32)
    sh["ident"] = np.eye(128, dtype=f32)
    inp32 = {k: np.asarray(v, f32) for k, v in inp.items()}
    for pi, (k, l) in enumerate(passes):
        sh["w%d" % pi] = pass_weights(k, l, inp32)
    kinds = [k for k, _ in passes]
    if "A" in kinds:
        ws = inp32["sg_w_s"]
        sh["wst"] = np.ascontiguousarray(ws.transpose(0, 3, 1, 2)).reshape(2, 128, 8 * 128)
        sh["maskT"] = np.triu(np.ones((128, 128), f32))
        sh["lngT"] = np.ascontiguousarray(inp32["sg_ln_g"].reshape(2, 16, 128).transpose(0, 2, 1))
        sh["lnbT"] = np.ascontiguousarray(inp32["sg_ln_b"].reshape(2, 16, 128).transpose(0, 2, 1))
        bs = inp32["sg_b_s"]
        bsj = np.repeat(bs, 2, axis=1)
        sh["bsj"] = np.ascontiguousarray(np.broadcast_to(bsj.reshape(2, 1, 16 * 128), (2, 128, 16 * 128)))
    if ("B" in kinds) or ("S" in kinds):
        g64 = np.array(GAMMA, np.float64)
        m_ = np.arange(128, dtype=np.float64)
        mask = (m_[:, None] <= m_[None, :]).astype(np.float64)
        dm = np.stack([mask * (g64[h] ** (-(m_ + 1.0)))[:, None] / 16.0 for h in range(H)], 1)
        sh["dmask"] = dm.reshape(128, H * 128).astype(f32)
        sh["epsp"] = np.stack([1e-5 / g64[h] ** (2.0 * (m_ + 1.0)) for h in range(H)], 1).astype(f32)
        sh["dk"] = np.stack([g64[h] ** (127.0 - m_) / 16.0 for h in range(H)], 1).astype(f32)
        sh["gng"] = np.ascontiguousarray(inp32["ret_gn_g"].reshape(2, 16, 128).transpose(0, 2, 1))
        sh["gnb"] = np.ascontiguousarray(inp32["ret_gn_b"].reshape(2, 16, 128).transpose(0, 2, 1))
    return sh


def core_inputs(inp, core, ntok, passes):
    f32 = np.float32
    b, half = core // 2, core % 2
    x = np.asarray(inp["x"], f32)
    t0 = half * TOK
    ci = {}
    ci["xT"] = np.ascontiguousarray(x[b, t0:t0 + ntok, :].T)
    ci["cT"] = np.ascontiguousarray(np.asarray(inp["c"], f32)[b].reshape(KT, 128).T)
    kinds = [k for k, _ in passes]
    if ("B" in kinds) or ("S" in kinds):
        cosT, sinT = rope_tables()
        ci["cosT"] = np.ascontiguousarray(cosT[:, t0:t0 + ntok])
        ci["sinT"] = np.ascontiguousarray(sinT[:, t0:t0 + ntok])
        ci["flag"] = np.full((128, 1), float(half), f32)
        ci["st_in"] = np.zeros((128, 8 * 512), f32)
    return ci


LAUNCHES = [
    [("A", 0), ("F", 0), ("S", 1)],
    [("B", 1), ("F", 1), ("A", 2), ("F", 2), ("S", 3)],
    [("B", 3), ("F", 3)],
]

_PROG_CACHE = {}


def _get_prog(passes):
    key = tuple(passes)
    if key not in _PROG_CACHE:
        _PROG_CACHE[key] = build_program(list(passes), ntok=TOK)
    return _PROG_CACHE[key]


FUSED_PASSES = [("A", 0), ("F", 0), ("S", 1), ("B", 1), ("F", 1), ("A", 2), ("F", 2), ("S", 3), ("B", 3), ("F", 3)]


def kernel(**inputs):
    inp = {k: np.asarray(v) for k, v in inputs.items()}
    key = ("fused",)
    if key not in _PROG_CACHE:
        _PROG_CACHE[key] = build_program(list(FUSED_PASSES), ntok=TOK, fused=True)
    nc = _PROG_CACHE[key]
    sh = shared_inputs(inp, FUSED_PASSES)
    in_maps = []
    for core in range(NCORES):
        ci = core_inputs(inp, core, TOK, FUSED_PASSES)
        ci.pop("st_in", None)
        ci.update(sh)
        in_maps.append(ci)
    res = run_bass_kernel_spmd(nc, in_maps, core_ids=list(range(NCORES)))
    out = np.empty((NB, SEQ, D), np.float32)
    for core in range(NCORES):
        b, half = core // 2, core % 2
        out[b, half * TOK:(half + 1) * TOK, :] = np.asarray(res.results[core]["outT"]).T
    return out


def kernel_unfused(**inputs):
    inp = {k: np.asarray(v) for k, v in inputs.items()}
    cur = None
    st = None
    for li, passes in enumerate(LAUNCHES):
        nc = _get_prog(passes)
        sh = shared_inputs(inp, passes)
        in_maps = []
        for core in range(NCORES):
            ci = core_inputs(inp, core, TOK, passes)
            if cur is not None:
                ci["xT"] = cur[core]
            if st is not None and "st_in" in ci:
                ci["st_in"] = st[core - 1] if core % 2 == 1 else np.zeros_like(st[core])
            ci.update(sh)
            in_maps.append(ci)
        res = run_bass_kernel_spmd(nc, in_maps, core_ids=list(range(NCORES)))
        cur = [np.asarray(r["outT"]) for r in res.results]
        if "st_out" in res.results[0]:
            st = [np.asarray(r["st_out"]) for r in res.results]
    out = np.empty((NB, SEQ, D), np.float32)
    for core in range(NCORES):
        b, half = core // 2, core % 2
        out[b, half * TOK:(half + 1) * TOK, :] = cur[core].T
    return out
```
